# Optimizing a Trainium2 kernel written in Bass

```python
import jax, jax.numpy as jnp
from jax import lax
import numpy as np

D_MODEL = 4096
BATCH = 32
SEQ = 256
DEPTH = 2
DEC_BATCH = 8
DEC_SEQ = 4096
PAST_LEN = 512

GRID_W = 64
N_HEADS = 32
N_KV_HEADS = 8
HEAD_DIM = D_MODEL // N_HEADS
GROUP = N_HEADS // N_KV_HEADS
Q_DIM = N_HEADS * HEAD_DIM
KV_DIM = N_KV_HEADS * HEAD_DIM
QKV_DIM = Q_DIM + 2 * KV_DIM
D_FF = 11008
CONV_WIDTH = 3
WINDOW = 128
Q_BLOCK = 128
N_NEIGH = -(-WINDOW // Q_BLOCK)
ROPE_THETA = 10000.0
AXIS_DIM = HEAD_DIM // 2
N_MOD = 6
EPS = 1e-6
N_A_LAYERS = (DEPTH + 1) // 2
N_B_LAYERS = DEPTH // 2

kernel_name = 'hybrid_diffusion_prefix_trunk_step'


def rms_norm(x, gain):
    xf = x.astype(jnp.float32)
    y = xf * lax.rsqrt(jnp.mean(xf * xf, axis=-1, keepdims=True) + EPS)
    return (y * gain.astype(jnp.float32)).astype(x.dtype)


def modulate(h, shift, scale):
    return h * (1 + scale) + shift


def grid_positions(n_tokens):
    rows = n_tokens // GRID_W
    row = jnp.repeat(jnp.arange(rows, dtype=jnp.int32), GRID_W)
    col = jnp.tile(jnp.arange(GRID_W, dtype=jnp.int32), rows)
    return row, col


def rope_axis(x, pos):
    half = AXIS_DIM // 2
    inv_freq = ROPE_THETA ** (-jnp.arange(half, dtype=jnp.float32) / half)
    ang = pos.astype(jnp.float32)[:, None] * inv_freq[None, :]
    cos = jnp.cos(ang)[None, :, None, :]
    sin = jnp.sin(ang)[None, :, None, :]
    xf = x.astype(jnp.float32)
    x1, x2 = xf[..., :half], xf[..., half:]
    return jnp.concatenate([x1 * cos - x2 * sin, x1 * sin + x2 * cos], axis=-1).astype(x.dtype)


def axial_rope(x, row, col):
    return jnp.concatenate([rope_axis(x[..., :AXIS_DIM], row), rope_axis(x[..., AXIS_DIM:], col)], axis=-1)


def softmax_with_sink(s, sink):
    if sink is None:
        return jax.nn.softmax(s, axis=-1)
    sk = sink[None, :, :, None, None]
    m = jnp.maximum(jnp.max(s, axis=-1, keepdims=True), sk)
    e = jnp.exp(s - m)
    return e / (jnp.sum(e, axis=-1, keepdims=True) + jnp.exp(sk - m))


def split_qkv(h, w_qkv):
    B, T, _ = h.shape
    qkv = h @ w_qkv
    q = qkv[..., :Q_DIM].reshape(B, T, N_HEADS, HEAD_DIM)
    k = qkv[..., Q_DIM:Q_DIM + KV_DIM].reshape(B, T, N_KV_HEADS, HEAD_DIM)
    v = qkv[..., Q_DIM + KV_DIM:].reshape(B, T, N_KV_HEADS, HEAD_DIM)
    return q, k, v


def to_query_blocks(q):
    B, T = q.shape[:2]
    qb = q.reshape(B, T // Q_BLOCK, Q_BLOCK, N_KV_HEADS, GROUP, HEAD_DIM)
    return qb.transpose(1, 0, 2, 3, 4, 5)


def from_query_blocks(o, B, T):
    return o.transpose(1, 0, 2, 3, 4, 5).reshape(B, T, Q_DIM)


def dense_attention(q, k, v, sink):
    B, T = q.shape[:2]
    scale = HEAD_DIM ** -0.5

    def one(qi):
        s = jnp.einsum('bqgrd,bkgd->bgrqk', qi, k).astype(jnp.float32) * scale
        p = softmax_with_sink(s, sink).astype(v.dtype)
        return jnp.einsum('bgrqk,bkgd->bqgrd', p, v)

    return from_query_blocks(lax.map(one, to_query_blocks(q)), B, T)


def banded_attention_with_context(q, k, v, k_ctx, v_ctx, sink):
    B, T = q.shape[:2]
    n_blocks = T // Q_BLOCK
    pad = N_NEIGH * Q_BLOCK
    span = (2 * N_NEIGH + 1) * Q_BLOCK
    n_ctx = k_ctx.shape[1]
    scale = HEAD_DIM ** -0.5
    kp = jnp.pad(k, ((0, 0), (pad, pad), (0, 0), (0, 0)))
    vp = jnp.pad(v, ((0, 0), (pad, pad), (0, 0), (0, 0)))

    def one(args):
        i, qi = args
        start = i * Q_BLOCK
        kw = lax.dynamic_slice_in_dim(kp, start, span, axis=1)
        vw = lax.dynamic_slice_in_dim(vp, start, span, axis=1)
        qpos = start + jnp.arange(Q_BLOCK)
        kpos = start - pad + jnp.arange(span)
        valid = (jnp.abs(qpos[:, None] - kpos[None, :]) <= WINDOW) & (kpos >= 0)[None, :] & (kpos < T)[None, :]
        s_ctx = jnp.einsum('bqgrd,bkgd->bgrqk', qi, k_ctx).astype(jnp.float32) * scale
        s_win = jnp.einsum('bqgrd,bkgd->bgrqk', qi, kw).astype(jnp.float32) * scale
        s_win = jnp.where(valid, s_win, -jnp.inf)
        p = softmax_with_sink(jnp.concatenate([s_ctx, s_win], axis=-1), sink).astype(v.dtype)
        return (jnp.einsum('bgrqk,bkgd->bqgrd', p[..., :n_ctx], v_ctx)
                + jnp.einsum('bgrqk,bkgd->bqgrd', p[..., n_ctx:], vw))

    out = lax.map(one, (jnp.arange(n_blocks), to_query_blocks(q)))
    return from_query_blocks(out, B, T)


def mixer_a_context(h, w_qkv, w_o, sink):
    q, k, v = split_qkv(h, w_qkv)
    o = dense_attention(q, k, v, sink)
    return o @ w_o, k, v


def mixer_a_latent(h, w_qkv, w_o, sink, k_ctx, v_ctx, row, col):
    q, k, v = split_qkv(h, w_qkv)
    q = axial_rope(q, row, col)
    k = axial_rope(k, row, col)
    o = banded_attention_with_context(q, k, v, k_ctx, v_ctx, sink)
    return o @ w_o


def mixer_b_context(h, w_qkv, w_o, q_gain, k_gain):
    q, k, v = split_qkv(h, w_qkv)
    q = rms_norm(q, q_gain)
    k = rms_norm(k, k_gain)
    o = dense_attention(q, k, v, None)
    return o @ w_o, k, v


def mixer_b_latent(h, w_qkv, w_o, q_gain, k_gain, k_ctx, v_ctx, row, col):
    q, k, v = split_qkv(h, w_qkv)
    q = axial_rope(rms_norm(q, q_gain), row, col)
    k = axial_rope(rms_norm(k, k_gain), row, col)
    k_all = jnp.concatenate([k_ctx, k], axis=1)
    v_all = jnp.concatenate([v_ctx, v], axis=1)
    o = dense_attention(q, k_all, v_all, None)
    return o @ w_o


def conv_ffn(h, w_gate, w_up, w_down, conv_w, conv_b):
    a = h @ w_gate
    u = h @ w_up
    ap = jnp.pad(a, ((0, 0), (1, 1), (0, 0)))
    a = ap[:, :-2] * conv_w[0] + ap[:, 1:-1] * conv_w[1] + ap[:, 2:] * conv_w[2] + conv_b
    return (jax.nn.silu(a) * u) @ w_down


def setup_inputs(seed: int = 0) -> dict:
    key = jax.random.key(seed)
    ks = jax.random.split(key, 24)
    f = jnp.float32
    nrm = lambda k, shape, s: jax.random.normal(k, shape, f) * s
    return {
        'x_prompt': nrm(ks[0], (BATCH, SEQ, D_MODEL), 1.0),
        'x_sample': nrm(ks[1], (DEC_BATCH, DEC_SEQ, D_MODEL), 1.0),
        'cache_k': nrm(ks[2], (DEC_BATCH, DEPTH, PAST_LEN, N_KV_HEADS, HEAD_DIM), 1.0),
        'cache_v': nrm(ks[3], (DEC_BATCH, DEPTH, PAST_LEN, N_KV_HEADS, HEAD_DIM), 1.0),
        'c': nrm(ks[4], (DEC_BATCH, D_MODEL), 1.0),
        'c_ctx': nrm(ks[5], (D_MODEL,), 1.0),
        'w_mod': nrm(ks[6], (DEPTH, D_MODEL, N_MOD * D_MODEL), D_MODEL ** -0.5),
        'b_mod': nrm(ks[7], (DEPTH, N_MOD * D_MODEL), 0.02),
        'norm_attn': 1.0 + nrm(ks[8], (DEPTH, D_MODEL), 0.02),
        'norm_ffn': 1.0 + nrm(ks[9], (DEPTH, D_MODEL), 0.02),
        'w_qkv': nrm(ks[10], (DEPTH, D_MODEL, QKV_DIM), D_MODEL ** -0.5),
        'w_o': nrm(ks[11], (DEPTH, Q_DIM, D_MODEL), Q_DIM ** -0.5),
        'sink_a': nrm(ks[12], (N_A_LAYERS, N_HEADS), 0.5),
        'q_norm_b': 1.0 + nrm(ks[13], (N_B_LAYERS, HEAD_DIM), 0.02),
        'k_norm_b': 1.0 + nrm(ks[14], (N_B_LAYERS, HEAD_DIM), 0.02),
        'w_gate': nrm(ks[15], (DEPTH, D_MODEL, D_FF), D_MODEL ** -0.5),
        'w_up': nrm(ks[16], (DEPTH, D_MODEL, D_FF), D_MODEL ** -0.5),
        'w_down': nrm(ks[17], (DEPTH, D_FF, D_MODEL), D_FF ** -0.5),
        'conv_w': nrm(ks[18], (DEPTH, CONV_WIDTH, D_FF), CONV_WIDTH ** -0.5),
        'conv_b': nrm(ks[19], (DEPTH, D_FF), 0.02),
        'norm_f': 1.0 + nrm(ks[20], (D_MODEL,), 0.02),
    }


def reference(x_prompt, x_sample, cache_k, cache_v, c, c_ctx, w_mod, b_mod, norm_attn, norm_ffn,
              w_qkv, w_o, sink_a, q_norm_b, k_norm_b, w_gate, w_up, w_down, conv_w, conv_b, norm_f):
    xp, xs = x_prompt, x_sample
    row, col = grid_positions(xs.shape[1])
    cond_s = jax.nn.silu(c)
    cond_p = jax.nn.silu(c_ctx)
    ks_out, vs_out = [], []
    for i in range(DEPTH):
        mod_p = cond_p @ w_mod[i] + b_mod[i]
        mod_s = (cond_s @ w_mod[i] + b_mod[i])[:, None, :]
        sh_ap, sc_ap, g_ap, sh_fp, sc_fp, g_fp = jnp.split(mod_p, N_MOD, axis=-1)
        sh_as, sc_as, g_as, sh_fs, sc_fs, g_fs = jnp.split(mod_s, N_MOD, axis=-1)
        k_ctx_in, v_ctx_in = cache_k[:, i], cache_v[:, i]
        hp = modulate(rms_norm(xp, norm_attn[i]), sh_ap, sc_ap)
        hs = modulate(rms_norm(xs, norm_attn[i]), sh_as, sc_as)
        if i % 2 == 0:
            sink = sink_a[i // 2].astype(jnp.float32).reshape(N_KV_HEADS, GROUP)
            op, kc, vc = mixer_a_context(hp, w_qkv[i], w_o[i], sink)
            osm = mixer_a_latent(hs, w_qkv[i], w_o[i], sink, k_ctx_in, v_ctx_in, row, col)
        else:
            qg, kg = q_norm_b[i // 2], k_norm_b[i // 2]
            op, kc, vc = mixer_b_context(hp, w_qkv[i], w_o[i], qg, kg)
            osm = mixer_b_latent(hs, w_qkv[i], w_o[i], qg, kg, k_ctx_in, v_ctx_in, row, col)
        ks_out.append(kc)
        vs_out.append(vc)
        xp = xp + g_ap * op
        xs = xs + g_as * osm
        hp = modulate(rms_norm(xp, norm_ffn[i]), sh_fp, sc_fp)
        hs = modulate(rms_norm(xs, norm_ffn[i]), sh_fs, sc_fs)
        xp = xp + g_fp * conv_ffn(hp, w_gate[i], w_up[i], w_down[i], conv_w[i], conv_b[i])
        xs = xs + g_fs * conv_ffn(hs, w_gate[i], w_up[i], w_down[i], conv_w[i], conv_b[i])
    y_prompt = rms_norm(xp, norm_f)
    y_sample = rms_norm(xs, norm_f)
    ctx_k = jnp.stack(ks_out, axis=1)
    ctx_v = jnp.stack(vs_out, axis=1)
    return (y_prompt, y_sample, ctx_k, ctx_v)
```

```python
import math
import numpy as np
import concourse.bass as bass
import concourse.mybir as mybir
from concourse.bass_utils import run_bass_kernel_spmd

F32 = mybir.dt.float32
BF16 = mybir.dt.bfloat16
AF = mybir.ActivationFunctionType
ALU = mybir.AluOpType
AX = mybir.AxisListType

EPS = 1e-6
ROPE_THETA = 10000.0
GRID_W = 64
WINDOW = 128
NCORES = 8
INF = float("inf")


class Cfg:
    def __init__(s, D, NH, NKV, DFF, SEQP, NPSEQ, S, PAST, DEPTH=2):
        s.D, s.NH, s.NKV, s.DFF, s.SEQP, s.NPSEQ, s.S, s.PAST, s.DEPTH = D, NH, NKV, DFF, SEQP, NPSEQ, S, PAST, DEPTH
        s.KC = D // 128
        s.FC = DFF // 128
        s.NFP = 3
        s.FH = (s.FC + s.NFP - 1) // s.NFP
        s.TP = SEQP * NPSEQ
        s.TT = s.TP + S
        s.NB = s.TT // 512
        s.NBP = s.TP // 512
        s.NCH = NH + NKV
        s.KVW = NKV * 128
        s.CBW = min(512, s.KVW)
        s.NCB = s.KVW // s.CBW
        s.KG = min(8, s.KC)
        s.NKG = s.KC // s.KG
        s.KM = min(16, s.KC)
        s.NKM = s.KC // s.KM
        s.NPC = PAST // 128
        s.NSC = S // 128
        assert NH // NKV == 4 and s.TP % 512 == 0 and S % 1024 == 0 and NH == s.KC
        assert 512 % SEQP == 0


class Eng:
    def __init__(s, name):
        s.name, s.items, s.count, s.waited = name, [], 0, {}


class Buf:
    def __init__(s, name):
        s.name = name
        s.w = {}
        s.r = {}
        s.semkey = None
        s.excl = False
        s.semcount = 0
        s.last_dma = None


class Group:
    def __init__(s, key):
        s.key, s.total = key, 0


class Prog:
    def __init__(s, nc):
        s.nc = nc
        s.engs = {n: Eng(n) for n in ("pe", "act", "dve", "pool", "sp")}
        s.sems = {}
        for n in s.engs:
            s.sems[("E", n)] = nc.alloc_semaphore(name="e_" + n)
        s.bufs = []
        s.groups = {}
        s.nsem = 0
        s.tag = ""
        s.sb_off = (int(nc.sbuf_base) + 63) // 64 * 64
        s.sb_top = int(nc.sbuf_top)

    def sbuf(s, name, shape, dtype, off=None):
        esz = 4 if dtype == F32 else 2
        n = 1
        for d in shape[1:]:
            n *= d
        nbytes = (n * esz + 63) // 64 * 64
        if off is None:
            off = s.sb_off
            s.sb_off += nbytes
            assert s.sb_off <= s.sb_top, ("SBUF overflow", name, s.sb_off, s.sb_top)
        else:
            assert off + nbytes <= s.sb_top, ("SBUF overflow", name)
        return s.nc.alloc_sbuf_tensor_at(name, list(shape), dtype, offset=off)

    def buf(s, name, dma=False):
        b = Buf(name)
        if dma:
            s.nsem += 1
            b.semkey = ("S", s.nsem)
            s.sems[b.semkey] = s.nc.alloc_semaphore(name="d_%d" % s.nsem)
        s.bufs.append(b)
        return b

    def group(s, name):
        g = Group(("G", name))
        s.sems[g.key] = s.nc.alloc_semaphore(name="g_" + name)
        s.groups[g.key] = g
        return g

    @staticmethod
    def _deps(reads, writes):
        d = {}
        for b in reads:
            for k, v in b.w.items():
                if d.get(k, 0) < v:
                    d[k] = v
            if b.excl:
                for k, v in b.r.items():
                    if d.get(k, 0) < v:
                        d[k] = v
        for b in writes:
            for k, v in b.w.items():
                if d.get(k, 0) < v:
                    d[k] = v
            for k, v in b.r.items():
                if d.get(k, 0) < v:
                    d[k] = v
        return d

    @staticmethod
    def _filter(E, deps, skip=None):
        out = []
        for k, v in deps.items():
            if k == skip:
                continue
            if E.waited.get(k, 0) >= v:
                continue
            E.waited[k] = v
            out.append((k, v))
        return out

    @staticmethod
    def _mark(tok, reads, writes):
        k, v = tok
        for b in writes:
            b.w = {k: v}
            b.r = {}
        for b in reads:
            if b not in writes:
                b.r[k] = v

    def op(s, e, fns, reads=(), writes=()):
        if not isinstance(fns, (list, tuple)):
            fns = [fns]
        E = s.engs[e]
        waits = s._filter(E, s._deps(reads, writes), skip=("E", "pe") if e == "pe" else None)
        E.count += 1
        tok = (("E", e), E.count)
        n = len(fns)
        for i, f in enumerate(fns):
            E.items.append((waits if i == 0 else (), f, (tok[0], 1) if i == n - 1 else None, s.tag))
        s._mark(tok, reads, writes)
        return tok

    def dma(s, q, out, in_, reads, writes, sembuf=None, group=None, **kw):
        E = s.engs[q]
        deps = s._deps(reads, writes)
        if group is not None:
            deps.pop(group.key, None)
            group.total += 16
            tok = (group.key, INF)
            semkey = group.key
        else:
            if sembuf.last_dma is not None:
                k, v = sembuf.last_dma
                if deps.get(k, 0) < v:
                    deps[k] = v
            sembuf.semcount += 16
            tok = (sembuf.semkey, sembuf.semcount)
            sembuf.last_dma = tok
            semkey = sembuf.semkey
        waits = s._filter(E, deps)
        E.items.append((waits, lambda eng: eng.dma_start(out=out, in_=in_, **kw), (semkey, 16), s.tag))
        s._mark(tok, reads, writes)
        return tok

    def barrier(s, final=False):
        deps = {}
        if final:
            for k, g in s.groups.items():
                if g.total:
                    deps[k] = INF
        for n, E in s.engs.items():
            if E.count:
                deps[("E", n)] = E.count
        for b in s.bufs:
            if b.semkey is not None and b.semcount:
                deps[b.semkey] = b.semcount
        for n, E in s.engs.items():
            w = s._filter(E, dict(deps), skip=("E", n))
            if w:
                E.items.append((w, None, None, "barrier"))
        for b in s.bufs:
            b.w = {k: v for k, v in b.w.items() if k[0] == "G"}
            b.r = {}

    def replay(s, block):
        nc = s.nc

        def run(E, eng):
            for waits, fn, sig, _tag in E.items:
                for k, v in waits:
                    if v == INF:
                        v = s.groups[k].total
                    eng.wait_ge(s.sems[k], int(v))
                if fn is None:
                    continue
                inst = fn(eng)
                if sig is not None:
                    inst.then_inc(s.sems[sig[0]], sig[1])

        @block.tensor
        def _(eng):
            run(s.engs["pe"], eng)

        @block.scalar
        def _(eng):
            run(s.engs["act"], eng)

        @block.vector
        def _(eng):
            run(s.engs["dve"], eng)

        @block.gpsimd
        def _(eng):
            run(s.engs["pool"], eng)

        @block.sync
        def _(eng):
            run(s.engs["sp"], eng)


class Ring:
    def __init__(s, items):
        s.items, s.i = items, 0

    def next(s):
        it = s.items[s.i % len(s.items)]
        s.i += 1
        return it


def build(cfg):
    nc = bass.Bass("TRN2", target_bir_lowering=False)
    P = Prog(nc)
    c = cfg
    D, KC, NH, NKV, FC, FH, TT, TP, S, NB, NBP = c.D, c.KC, c.NH, c.NKV, c.FC, c.FH, c.TT, c.TP, c.S, c.NB, c.NBP
    DEPTH, NCH, KVW, CBW, NCB, KG, NKG, KM, NKM, PAST = c.DEPTH, c.NCH, c.KVW, c.CBW, c.NCB, c.KG, c.NKG, c.KM, c.NKM, c.PAST
    scale = 128 ** -0.5

    def din(name, shape, dt=F32):
        return nc.dram_tensor(name, list(shape), dt, kind="ExternalInput").ap()

    def dout(name, shape, dt=F32):
        return nc.dram_tensor(name, list(shape), dt, kind="ExternalOutput").ap()

    def dint(name, shape, dt):
        return nc.dram_tensor(name, list(shape), dt, kind="Internal").ap()

    xT = din("xT", [D, TT])
    cT = din("cT", [128, KC * 2])
    wmod = din("wmod", [DEPTH * 6 * KC * 128, D])
    bmod = din("bmod", [128, DEPTH * 6 * KC])
    nrm = din("nrm", [128, (2 * DEPTH + 1) * KC])
    wqk = din("wqk", [DEPTH * NCH * 128, D])
    wkvt = din("wkvt", [DEPTH * 2 * NCB * NKG * 128, KG * CBW])
    wo = din("wo", [DEPTH * KC * 128, D])
    wg = din("wg", [DEPTH * FC * 128, D])
    wu = din("wu", [DEPTH * FC * 128, D])
    wd = din("wd", [DEPTH * KC * 128, FC * 128])
    convw = din("convw", [128, DEPTH * 3 * FC])
    convb = din("convb", [128, DEPTH * FC])
    sinkd = din("sink", [128, NH])
    qkg = din("qkg", [128, 2])
    kgb = din("kgb", [128, 128])
    ropeC = din("ropeC", [128, S])
    ropeS = din("ropeS", [128, S])
    kctxT = din("kctxT", [DEPTH * KVW, PAST])
    vctx = din("vctx", [DEPTH * PAST, KVW])
    masksd = din("masks", [128, 1024])
    permd = din("perm", [128, 128])
    yT = dout("yT", [D, TT])
    ctxk = dout("ctxk", [DEPTH * TP, KVW])
    ctxv = dout("ctxv", [DEPTH * TP, KVW])

    wqk_b = dint("wqk_b", [DEPTH * NCH * 128, D], BF16)
    wkvt_b = dint("wkvt_b", [DEPTH * 2 * NCB * NKG * 128, KG * CBW], BF16)
    wo_b = dint("wo_b", [DEPTH * KC * 128, D], BF16)
    wg_b = dint("wg_b", [DEPTH * FC * 128, D], BF16)
    wu_b = dint("wu_b", [DEPTH * FC * 128, D], BF16)
    wd_b = dint("wd_b", [DEPTH * KC * 128, FC * 128], BF16)
    kctxT_b = dint("kctxT_b", [DEPTH * KVW, PAST], BF16)
    vctx_b = dint("vctx_b", [DEPTH * PAST, KVW], BF16)
    xs_d = [xT] + [dint("xres%d" % i, [D, TT], F32) for i in range(2 * DEPTH)]
    qT_d = [dint("qT%d" % l, [NH * 128, TT], BF16) for l in range(DEPTH)]
    kT_d = [dint("kT%d" % l, [KVW, TT], BF16) for l in range(DEPTH)]
    v_d = [dint("v%d" % l, [TT, KVW], BF16) for l in range(DEPTH)]
    oT_d = [dint("oT%d" % l, [NH * 128, TT], BF16) for l in range(DEPTH)]

    dbufs = {}

    def dbuf(name, blk=0):
        k = (name, blk)
        if k not in dbufs:
            dbufs[k] = P.buf("dram_%s_%d" % (name, blk))
        return dbufs[k]

    ps = nc.alloc_psum_tensor("ps", [128, 8, 512], F32)
    psb = [P.buf("psum%d" % i) for i in range(8)]
    for b_ in psb:
        b_.excl = True

    def tile(name, shape, dt, dma=False, off=None):
        return P.sbuf(name, shape, dt, off), P.buf(name, dma=dma)

    ones_t, ones_b = tile("ones", [128, 128], BF16)
    perm_t, perm_b = tile("perm", [128, 128], BF16)
    mask_t, mask_b = tile("mask", [128, 1024], BF16)
    zero_t, zero_b = tile("zero", [128, 128], F32)
    onesf_t, onesf_b = tile("onesf", [128, 128], F32)
    cgrp = P.group("const")
    c_t, c_b = tile("cT", [128, KC, 2], F32)
    csil_t, csil_b = tile("csil", [128, KC, 2], F32)
    bmod_t, bmod_b = tile("bmod", [128, DEPTH, 6 * KC], F32)
    nrm_t, nrm_b = tile("nrm", [128, 2 * DEPTH + 1, KC], F32)
    convw_t, convw_b = tile("convw", [128, DEPTH, 3, FC], F32)
    convb_t, convb_b = tile("convb", [128, DEPTH, FC], F32)
    sink_t, sink_b = tile("sink", [128, NH], F32)
    qkg_t, qkg_b = tile("qkg", [128, 2], F32)
    kgb_t, kgb_b = tile("kgb", [128, 128], F32)
    modT_t, modT_b = tile("modT", [128, DEPTH, 2, 6 * KC], F32)
    eff_t, eff_b = tile("eff", [128, DEPTH, 2, 2, KC], F32)
    negB_t, negB_b = tile("negB", [128, 1], F32)
    se_t, se_b = tile("se", [128, NH], F32)

    WSLOT = max(KC * 128, FH * 128, KG * CBW)
    wring = Ring([tile("w%d" % i, [128, WSLOT], BF16, dma=True) for i in range(5)])
    xsl = Ring([tile("xsl%d" % i, [128, 514], F32, dma=True) for i in range(3)])
    sq = Ring([tile("sq%d" % i, [128, 512], BF16) for i in range(2)])
    f32a = Ring([tile("fa%d" % i, [128, 516], F32) for i in range(4)])
    f32b = Ring([tile("fb%d" % i, [128, 512], F32) for i in range(3)])
    bfa = Ring([tile("ba%d" % i, [128, 512], BF16) for i in range(3)])
    stg_b16 = Ring([tile("sb%d" % i, [128, 512], BF16, dma=True) for i in range(3)])
    stg_f32 = Ring([tile("sf%d" % i, [128, 512], F32, dma=True) for i in range(3)])
    rope_r = Ring([(tile("rc%d" % i, [128, 512], F32, dma=True), tile("rs%d" % i, [128, 512], F32, dma=True)) for i in range(2)])
    t1r = Ring([tile("t1r%d" % i, [128, 512], F32) for i in range(3)])
    small_t, small_b = tile("small", [128, 2 * KC + 64], F32)
    small2_t, small2_b = tile("small2", [128, 2 * KC + 32], F32)
    SM = 2 * KC

    BIG = P.sb_off
    HTB = ((KC * 516 * 2 + 63) // 64) * 64
    hT = [(P.sbuf("hT%d" % i, [128, KC, 516], BF16, BIG + i * HTB), P.buf("hT%d" % i, dma=True)) for i in range(2)]
    gT_t = P.sbuf("gT", [128, FH, 512], BF16, BIG + 2 * HTB)
    gT_b = P.buf("gT")
    assert BIG + 2 * HTB + FH * 1024 <= P.sb_top, ("BIG overflow", BIG, HTB, FH, P.sb_top)
    wm = Ring([(P.sbuf("wm%d" % i, [128, KM, 128], F32, BIG + i * KM * 512), P.buf("wm%d" % i, dma=True)) for i in range(2)])
    KL = PAST + max(S, TP)
    o = BIG
    attQ = []
    for i in range(2):
        attQ.append((P.sbuf("aQ%d" % i, [128, 4, 1024], BF16, o), P.buf("aQ%d" % i, dma=True)))
        o += 8192
    attK = []
    for i in range(2):
        attK.append((P.sbuf("aK%d" % i, [128, KL], BF16, o), P.buf("aK%d" % i, dma=True)))
        o += (KL * 2 + 63) // 64 * 64
    attV = []
    for i in range(2):
        attV.append((P.sbuf("aV%d" % i, [128, KL // 128, 128], BF16, o), P.buf("aV%d" % i, dma=True)))
        o += (KL * 2 + 63) // 64 * 64
    ptr = []
    for i in range(3):
        ptr.append((P.sbuf("pt%d" % i, [128, 512], BF16, o), P.buf("pt%d" % i)))
        o += 1024
    ostg = []
    for i in range(2):
        ostg.append((P.sbuf("os%d" % i, [128, 4, 512], BF16, o), P.buf("os%d" % i, dma=True)))
        o += 4096
    sexp_t = P.sbuf("sexp", [128, 512], F32, o)
    sexp_b = P.buf("sexp")
    o += 2048
    assert o <= P.sb_top, "attention overlay overflow"

    def mm(out, lhsT, rhs, start, stop):
        return lambda e: e.matmul(out, lhsT=lhsT, rhs=rhs, start=start, stop=stop)

    def act(out, in_, func, reads, writes, bias=None, scale=None):
        kw = {}
        if bias is not None:
            kw["bias"] = bias
        if scale is not None:
            kw["scale"] = scale
        P.op("act", lambda e: e.activation(out=out, in_=in_, func=func, **kw), reads, writes)

    def ts(eng, out, in0, s1, s2, op0, op1, reads, writes):
        if op1 is None:
            P.op(eng, lambda e: e.tensor_scalar(out=out, in0=in0, scalar1=s1, scalar2=None, op0=op0), reads, writes)
        else:
            P.op(eng, lambda e: e.tensor_scalar(out=out, in0=in0, scalar1=s1, scalar2=s2, op0=op0, op1=op1), reads, writes)

    def stt(out, in0, scalar, in1, op0, op1, reads, writes):
        P.op("dve", lambda e: e.scalar_tensor_tensor(out=out, in0=in0, scalar=scalar, in1=in1, op0=op0, op1=op1), reads, writes)

    def tt(eng, out, in0, in1, op, reads, writes):
        P.op(eng, lambda e: e.tensor_tensor(out=out, in0=in0, in1=in1, op=op), reads, writes)

    groups = {}
    bg = []

    def add_cast(gname, dst, src, rows, dbname):
        if gname not in groups:
            groups[gname] = P.group(gname)
        g = groups[gname]
        R = dst.shape[0]
        for r0 in range(0, R, rows):
            r1 = min(R, r0 + rows)
            bg.append((g, dst[r0:r1, :], src[r0:r1, :], dbname))

    import os as _os2
    KDBG = int(_os2.environ.get("KDBG", "0"))

    def bg_step(n=1):
        if KDBG & 1:
            bg.clear()
            return
        for _ in range(n):
            if not bg:
                return
            g, dst, src, dbname = bg.pop(0)
            P.dma("pool", dst, src, [], [dbuf(dbname)], group=g)

    def bg_step_l0():
        if bg and (bg[0][0].key[1].endswith("0") or bg[0][0].key[1] == "ctx"):
            bg_step(1)

    def bg_flush_until(gname):
        while any(g is groups[gname] for g, _, _, _ in bg):
            bg_step()

    for l in range(DEPTH):
        add_cast("wqk%d" % l, wqk_b[l * NCH * 128:(l + 1) * NCH * 128, :], wqk[l * NCH * 128:(l + 1) * NCH * 128, :], 128, "wqk_b%d" % l)
        n = 2 * NCB * NKG * 128
        add_cast("wqk%d" % l, wkvt_b[l * n:(l + 1) * n, :], wkvt[l * n:(l + 1) * n, :], 128, "wqk_b%d" % l)
        if l == 0:
            add_cast("ctx", kctxT_b, kctxT, 128, "ctx_b")
            add_cast("ctx", vctx_b, vctx, 128, "ctx_b")
        add_cast("wo%d" % l, wo_b[l * KC * 128:(l + 1) * KC * 128, :], wo[l * KC * 128:(l + 1) * KC * 128, :], 128, "wo_b%d" % l)
        add_cast("wgu%d" % l, wg_b[l * FC * 128:(l + 1) * FC * 128, :], wg[l * FC * 128:(l + 1) * FC * 128, :], 128, "wgu_b%d" % l)
        add_cast("wgu%d" % l, wu_b[l * FC * 128:(l + 1) * FC * 128, :], wu[l * FC * 128:(l + 1) * FC * 128, :], 128, "wgu_b%d" % l)
        add_cast("wd%d" % l, wd_b[l * KC * 128:(l + 1) * KC * 128, :], wd[l * KC * 128:(l + 1) * KC * 128, :], 128, "wd_b%d" % l)

    def cload(t, b, src):
        P.dma("sp", t, src, [], [b], group=cgrp)

    cload(c_t[:], c_b, cT.rearrange("p (k c) -> p k c", c=2))
    cload(bmod_t[:], bmod_b, bmod.rearrange("p (l m) -> p l m", l=DEPTH))
    cload(nrm_t[:], nrm_b, nrm.rearrange("p (n k) -> p n k", k=KC))
    cload(convw_t[:], convw_b, convw.rearrange("p (l t f) -> p l t f", l=DEPTH, t=3))
    cload(convb_t[:], convb_b, convb.rearrange("p (l f) -> p l f", l=DEPTH))
    cload(sink_t[:], sink_b, sinkd)
    cload(qkg_t[:], qkg_b, qkg)
    cload(kgb_t[:], kgb_b, kgb)
    P.dma("pool", perm_t[:], permd, [], [perm_b], group=cgrp)
    P.dma("pool", mask_t[:], masksd, [], [mask_b], group=cgrp)
    P.op("dve", lambda e: e.memset(ones_t[:], 1.0), [], [ones_b])
    P.op("dve", lambda e: e.memset(zero_t[:], 0.0), [], [zero_b])
    P.op("dve", lambda e: e.memset(onesf_t[:], 1.0), [], [onesf_b])
    P.op("dve", lambda e: e.memset(negB_t[:], 0.0), [], [negB_b])
    bg_flush_until("wqk0")
    bg_flush_until("ctx")

    act(csil_t[:], c_t[:], AF.Silu, [c_b], [csil_b])
    for l in range(DEPTH if not (KDBG & 2) else 0):
        pb = psb[l]
        for ch in range(6 * KC):
            for hm in range(NKM):
                (wt, wb) = wm.next()
                r0 = (l * 6 * KC + ch) * 128
                P.dma("sp", wt[:], wmod[r0:r0 + 128, hm * KM * 128:(hm + 1) * KM * 128].rearrange("p (k j) -> p k j", j=128), [], [wb], sembuf=wb)
                fns = [mm(ps[:, l, ch * 2:ch * 2 + 2], wt[:, k, :], csil_t[:, hm * KM + k, :], (hm == 0 and k == 0), (hm == NKM - 1 and k == KM - 1)) for k in range(KM)]
                P.op("pe", fns, [wb, csil_b], [pb])
            bg_step(1 if ch % 2 == 0 else 0)
        for cd in range(2):
            tt("dve", modT_t[:, l, cd, :], ps[:, l, 0:12 * KC].rearrange("p (m c) -> p m c", c=2)[:, :, cd], bmod_t[:, l, :], ALU.add, [pb, bmod_b], [modT_b])
            for af in range(2):
                stt(eff_t[:, l, cd, af, :], modT_t[:, l, cd, (3 * af + 1) * KC:(3 * af + 2) * KC], 1.0, nrm_t[:, 2 * l + af, :], ALU.add, ALU.mult, [modT_b, nrm_b], [eff_b])
    bg_flush_until("wo0")
    P.barrier()

    def effs(l, cd, af, k):
        return eff_t[:, l, cd, af, k:k + 1]

    def shs(l, cd, af, k):
        return modT_t[:, l, cd, 3 * af * KC + k:3 * af * KC + k + 1]

    def gates(l, cd, af, k):
        return modT_t[:, l, cd, (3 * af + 2) * KC + k:(3 * af + 2) * KC + k + 1]

    PS_PREP = 4
    PS_PREPH = 3

    def prep(src, srcname, blk, l, af, hbuf, halo_l=False, halo_r=False, final=False):
        ht, hb = hbuf
        cd = 0 if blk < NBP else 1
        c0 = blk * 512
        lo, hi = c0 - (1 if halo_l else 0), c0 + 512 + (1 if halo_r else 0)
        so = 0 if halo_l else 1
        W = hi - lo
        def rbk(k):
            r = [dbuf(srcname, blk * KC + k)]
            if halo_l:
                r.append(dbuf(srcname, (blk - 1) * KC + k))
            if halo_r:
                r.append(dbuf(srcname, (blk + 1) * KC + k))
            return r
        halos = [(0, halo_l), (513, halo_r)]
        hcol = {0: 0, 513: 514}
        anyh = halo_l or halo_r
        for k in range(KC):
            xt, xb = xsl.next()
            P.dma("sp", xt[:, so:so + W], src[k * 128:(k + 1) * 128, lo:hi], rbk(k), [xb], sembuf=xb)
            st, sb_ = sq.next()
            act(st[:], xt[:, 1:513], AF.Square, [xb], [sb_])
            P.op("pe", mm(ps[:, PS_PREP, :], ones_t[:], st[:], k == 0, k == KC - 1), [sb_, ones_b], [psb[PS_PREP]])
            if anyh:
                for hi_, (col, on) in enumerate(halos):
                    if on:
                        act(small_t[:, 2 * k + hi_:2 * k + hi_ + 1], xt[:, col:col + 1], AF.Square, [xb], [small_b])
            yield
        rt, rtb = f32b.next()
        act(rt[:], ps[:, PS_PREP, :], AF.Sqrt, [psb[PS_PREP]], [rtb], bias=EPS, scale=1.0 / D)
        P.op("dve", lambda e: e.reciprocal(out=ps[:, PS_PREP, :], in_=rt[:]), [rtb], [psb[PS_PREP]])
        if anyh:
            for hi_, (col, on) in enumerate(halos):
                if on:
                    P.op("dve", lambda e, hi_=hi_: e.tensor_copy(out=small2_t[:, hi_ * KC:(hi_ + 1) * KC], in_=small_t[:, hi_:2 * KC:2]), [small_b], [small2_b])
                    P.op("dve", lambda e, hi_=hi_: e.tensor_reduce(out=small2_t[:, 2 * KC + 8 + hi_:2 * KC + 9 + hi_], in_=small2_t[:, hi_ * KC:(hi_ + 1) * KC], axis=AX.X, op=ALU.add), [small2_b], [small2_b])
                    P.op("pe", mm(ps[:, PS_PREPH, hi_:hi_ + 1], onesf_t[:], small2_t[:, 2 * KC + 8 + hi_:2 * KC + 9 + hi_], True, True), [small2_b, onesf_b], [psb[PS_PREPH]])
                    act(small2_t[:, 2 * KC + hi_:2 * KC + hi_ + 1], ps[:, PS_PREPH, hi_:hi_ + 1], AF.Sqrt, [psb[PS_PREPH]], [small2_b], bias=EPS, scale=1.0 / D)
                    P.op("dve", lambda e, hi_=hi_: e.reciprocal(out=small2_t[:, 2 * KC + 2 + hi_:2 * KC + 3 + hi_], in_=small2_t[:, 2 * KC + hi_:2 * KC + hi_ + 1]), [small2_b], [small2_b])
        yield
        for k in range(KC):
            xt, xb = xsl.next()
            P.dma("sp", xt[:, so:so + W], src[k * 128:(k + 1) * 128, lo:hi], rbk(k), [xb], sembuf=xb)
            if final:
                ot, ob = stg_f32.next()
                stt(ot[:], xt[:, 1:513], nrm_t[:, 2 * DEPTH, k:k + 1], ps[:, PS_PREP, :], ALU.mult, ALU.mult, [xb, psb[PS_PREP], nrm_b], [ob])
                P.dma("pool", yT[k * 128:(k + 1) * 128, c0:c0 + 512], ot[:], [ob], [dbuf("yT", blk)], sembuf=ob)
            else:
                tt_, tb = f32a.next()
                stt(tt_[:, 0:512], xt[:, 1:513], effs(l, cd, af, k), ps[:, PS_PREP, :], ALU.mult, ALU.mult, [xb, psb[PS_PREP], eff_b], [tb])
                act(ht[:, k, 2:514], tt_[:, 0:512], AF.Identity, [tb, modT_b], [hb], bias=shs(l, cd, af, k))
                for hi_, (col, on) in enumerate(halos):
                    if on:
                        P.op("dve", lambda e, hi_=hi_, col=col, k=k, xt=xt: e.tensor_scalar(
                            out=small_t[:, SM + hi_:SM + 1 + hi_], in0=xt[:, col:col + 1], scalar1=effs(l, cd, af, k),
                            scalar2=small2_t[:, 2 * KC + 2 + hi_:2 * KC + 3 + hi_], op0=ALU.mult, op1=ALU.mult), [xb, small2_b, eff_b], [small_b])
                        act(ht[:, k, hcol[col]:hcol[col] + 1], small_t[:, SM + hi_:SM + 1 + hi_], AF.Identity, [small_b, modT_b], [hb], bias=shs(l, cd, af, k))
            yield
        if not final:
            for hi_, (col, on) in enumerate(halos):
                if not on:
                    P.op("pool", lambda e, col=col: e.memset(ht[:, :, hcol[col]:hcol[col] + 1], 0.0), [], [hb])

    def drive(gen):
        for _ in gen:
            pass

    PS_A = [0, 1]
    PS_ROT = 2
    PS_SS = 3
    PS_V = [5, 6, 7, 2]

    def load_w(src_ap, nel, deps_db):
        wt, wb = wring.next()
        P.dma("sp", wt[:, 0:nel], src_ap, [deps_db], [wb], sembuf=wb)
        return wt, wb

    def phase_A(l):
        src, srcname = xs_d[2 * l], "x%d" % (2 * l)
        wdb = dbuf("wqk_b%d" % l)
        gen = prep(src, srcname, 0, l, 0, hT[0])
        drive(gen)
        KA = int(_os2.environ.get("KA", "0"))
        if KA == 1:
            P.barrier()
            return
        for blk in range(NB):
            ht, hb = hT[blk % 2]
            nxt = prep(src, srcname, blk + 1, l, 0, hT[(blk + 1) % 2]) if blk + 1 < NB else None
            sample = blk >= NBP and KA not in (3, 4)
            c0 = blk * 512
            if sample:
                (rc_t, rc_b), (rs_t, rs_b) = rope_r.next()
                s0 = c0 - TP
                P.dma("sp", rc_t[:], ropeC[:, s0:s0 + 512], [], [rc_b], sembuf=rc_b)
                P.dma("sp", rs_t[:], ropeS[:, s0:s0 + 512], [], [rs_b], sembuf=rs_b)
            PSA3 = [0, 1, 5]
            pend1, pend2 = [], []

            def stage0(ch):
                r0 = (l * NCH + ch) * 128
                wt, wb = load_w(wqk_b[r0:r0 + 128, :], KC * 128, wdb)
                pa = PSA3[ch % 3]
                fns = [mm(ps[:, pa, :], wt[:, k * 128:(k + 1) * 128], ht[:, k, 2:514], k == 0, k == KC - 1) for k in range(KC)]
                P.op("pe", fns, [wb, hb], [psb[pa]])
                return dict(ch=ch, pa=pa)

            def stage1(st_):
                ch, pa = st_["ch"], st_["pa"]
                isq = ch < NH
                cur, cur_b = ps[:, pa, :], psb[pa]
                st_["ot"] = None
                if l % 2 == 1:
                    sqt, sqb = sq.next()
                    act(sqt[:], cur, AF.Square, [cur_b], [sqb])
                    P.op("pe", mm(ps[:, PS_SS, :], ones_t[:], sqt[:], True, True), [sqb, ones_b], [psb[PS_SS]])
                    rt, rtb = f32b.next()
                    act(rt[:], ps[:, PS_SS, :], AF.Sqrt, [psb[PS_SS]], [rtb], bias=EPS, scale=1.0 / 128)
                    rs2, rs2b = f32b.next()
                    P.op("dve", lambda e, rs2=rs2, rt=rt: e.reciprocal(out=rs2[:], in_=rt[:]), [rtb], [rs2b])
                    gcol = qkg_t[:, 0:1] if isq else qkg_t[:, 1:2]
                    if sample:
                        qn, qnb = f32a.next()
                        stt(qn[:, 0:512], cur, gcol, rs2[:], ALU.mult, ALU.mult, [cur_b, rs2b, qkg_b], [qnb])
                        cur, cur_b = qn[:, 0:512], qnb
                    else:
                        ot, ob = stg_b16.next()
                        stt(ot[:], cur, gcol, rs2[:], ALU.mult, ALU.mult, [cur_b, rs2b, qkg_b], [ob])
                        st_["ot"] = (ot, ob)
                if sample:
                    qb_t, qb_b = bfa.next()
                    act(qb_t[:], cur, AF.Copy, [cur_b], [qb_b])
                    t1, t1b = t1r.next()
                    tt("dve", t1[:], cur, rc_t[:], ALU.mult, [cur_b, rc_b, qb_b], [t1b])
                    st_["qb"] = (qb_t, qb_b)
                    st_["t1"] = (t1, t1b)
                elif l % 2 == 0:
                    ot, ob = stg_b16.next()
                    act(ot[:], cur, AF.Copy, [cur_b], [ob])
                    st_["ot"] = (ot, ob)

            def stage2(st_):
                ch = st_["ch"]
                isq = ch < NH
                dst = (qT_d[l][ch * 128:(ch + 1) * 128, c0:c0 + 512] if isq else kT_d[l][(ch - NH) * 128:(ch - NH + 1) * 128, c0:c0 + 512])
                dname = "qT%d" % l if isq else "kT%d" % l
                if sample:
                    qb_t, qb_b = st_["qb"]
                    t1, t1b = st_["t1"]
                    ot, ob = stg_b16.next()
                    P.op("pe", mm(ps[:, PS_ROT, :], perm_t[:], qb_t[:], True, True), [qb_b, perm_b], [psb[PS_ROT]])
                    tt("dve", ps[:, PS_ROT, :], ps[:, PS_ROT, :], rs_t[:], ALU.mult, [psb[PS_ROT], rs_b], [psb[PS_ROT]])
                    tt("dve", ot[:], ps[:, PS_ROT, :], t1[:], ALU.add, [t1b, psb[PS_ROT]], [ob])
                else:
                    ot, ob = st_["ot"]
                P.dma("pool", dst, ot[:], [ob], [dbuf(dname, blk)], sembuf=ob)

            for ch in range(NCH + 2):
                if ch < NCH:
                    pend1.append(stage0(ch))
                if ch >= 1 and pend1 and pend1[0]["ch"] == ch - 1:
                    st_ = pend1.pop(0)
                    stage1(st_)
                    pend2.append(st_)
                if ch >= 2 and pend2 and pend2[0]["ch"] == ch - 2:
                    stage2(pend2.pop(0))
                if nxt is not None and KA not in (4, 5):
                    next(nxt, None)
                    next(nxt, None)
                bg_step(1 if (l == 0 and ch % 3 == 0 and KA not in (4, 5)) else 0)
            assert not pend1 and not pend2
            for mat in ([] if KA in (2, 4, 5) else [0, 1] if blk < NBP else [1]):
                for cb in range(NCB):
                    for kg in range(NKG):
                        r0 = ((((l * 2 + mat) * NCB + cb) * NKG) + kg) * 128
                        wt, wb = load_w(wkvt_b[r0:r0 + 128, :], KG * CBW, wdb)
                        for t4 in range(4):
                            pv = PS_V[t4]
                            fns = [mm(ps[:, pv, 0:CBW], ht[:, kg * KG + k, 2 + t4 * 128:2 + (t4 + 1) * 128], wt[:, k * CBW:(k + 1) * CBW],
                                      (kg == 0 and k == 0), (kg == NKG - 1 and k == KG - 1)) for k in range(KG)]
                            P.op("pe", fns, [wb, hb], [psb[pv]])
                    for t4 in range(4):
                        pv = PS_V[t4]
                        tok0 = c0 + t4 * 128
                        if mat == 1:
                            ot, ob = stg_b16.next()
                            act(ot[:, 0:CBW], ps[:, pv, 0:CBW], AF.Copy, [psb[pv]], [ob])
                            P.dma("pool", v_d[l][tok0:tok0 + 128, cb * CBW:(cb + 1) * CBW], ot[:, 0:CBW], [ob], [dbuf("v%d" % l, blk)], sembuf=ob)
                        if blk < NBP:
                            of, ofb = stg_f32.next()
                            if mat == 0 and l % 2 == 1:
                                nh = CBW // 128
                                s2, s2b = f32a.next()
                                act(s2[:, 0:CBW], ps[:, pv, 0:CBW], AF.Square, [psb[pv]], [s2b])
                                P.op("dve", lambda e, s2=s2, nh=nh: e.tensor_reduce(out=small_t[:, SM + 8:SM + 8 + nh], in_=s2[:, 0:CBW].rearrange("p (h d) -> p h d", h=nh), axis=AX.X, op=ALU.add), [s2b], [small_b])
                                act(small_t[:, SM + 16:SM + 16 + nh], small_t[:, SM + 8:SM + 8 + nh], AF.Sqrt, [small_b], [small_b], bias=EPS, scale=1.0 / 128)
                                P.op("dve", lambda e, nh=nh: e.reciprocal(out=small_t[:, SM + 24:SM + 24 + nh], in_=small_t[:, SM + 16:SM + 16 + nh]), [small_b], [small_b])
                                for h in range(nh):
                                    stt(of[:, h * 128:(h + 1) * 128], ps[:, pv, h * 128:(h + 1) * 128], small_t[:, SM + 24 + h:SM + 25 + h], kgb_t[:], ALU.mult, ALU.mult, [psb[pv], small_b, kgb_b], [ofb])
                            else:
                                P.op("dve", lambda e, of=of, pv=pv: e.tensor_copy(out=of[:, 0:CBW], in_=ps[:, pv, 0:CBW]), [psb[pv]], [ofb])
                            od = ctxk if mat == 0 else ctxv
                            P.dma("pool", od[l * TP + tok0:l * TP + tok0 + 128, cb * CBW:(cb + 1) * CBW], of[:, 0:CBW], [ofb], [dbuf("ctxo", 0)], sembuf=ofb)
            if nxt is not None:
                drive(nxt)
        P.barrier()

    PS_S = [0, 1, 6]
    PS_O = [2, 3]
    PS_SUM = [4, 5]

    def phase_B(l):
        sinkon = (l % 2 == 0)
        windowed = (l % 2 == 0)
        if sinkon:
            act(se_t[:], sink_t[:], AF.Exp, [sink_b, negB_b], [se_b], bias=negB_t[:, 0:1])
        kbi = [0]

        def run_stream(qblocks):
            flat = []
            for qi, qbk in enumerate(qblocks):
                for j, t in enumerate(qbk["tiles"]):
                    flat.append((qi, j, t))
            n = len(flat)
            pts = [None] * n

            def emit_S(i):
                qi, j, (kap, vap, mk, bfs) = flat[i]
                qap, qbuf = qblocks[qi]["q"]
                P.op("pe", mm(ps[:, PS_S[i % 3], :], kap, qap, True, True), bfs + [qbuf], [psb[PS_S[i % 3]]])

            emit_S(0)
            if n > 1:
                emit_S(1)
            for i in range(n):
                qi, j, (kap, vap, mk, bfs) = flat[i]
                nt = len(qblocks[qi]["tiles"])
                if i + 2 < n:
                    emit_S(i + 2)
                pt, ptb = ptr[i % 3]
                act(pt[:], ps[:, PS_S[i % 3], :], AF.Exp, [psb[PS_S[i % 3]], negB_b], [ptb], bias=negB_t[:, 0:1], scale=scale)
                if mk is not None:
                    tt("pool", pt[:], pt[:], mk, ALU.mult, [ptb, mask_b], [ptb])
                po, psm = PS_O[qi % 2], PS_SUM[qi % 2]
                P.op("pe", [mm(ps[:, po, :], vap, pt[:], j == 0, j == nt - 1), mm(ps[:, psm, :], ones_t[:], pt[:], j == 0, j == nt - 1)],
                     bfs + [ptb, ones_b], [psb[po], psb[psm]])
                if j == nt - 1:
                    g = qblocks[qi]["g"]
                    den, denb = f32b.next()
                    if sinkon:
                        tt("dve", den[:], ps[:, psm, :], sexp_t[:], ALU.add, [psb[psm], sexp_b], [denb])
                    else:
                        P.op("dve", lambda e, den=den, psm=psm: e.tensor_copy(out=den[:], in_=ps[:, psm, :]), [psb[psm]], [denb])
                    rd, rdb = f32b.next()
                    P.op("dve", lambda e, rd=rd, den=den: e.reciprocal(out=rd[:], in_=den[:]), [denb], [rdb])
                    qblocks[qi]["fin"](po, rd, rdb)
                    if l == 0:
                        bg_step_l0()

        def set_sexp(g):
            if sinkon:
                for h in range(4):
                    ts("dve", sexp_t[:, h * 128:(h + 1) * 128], zero_t[:], se_t[:, 4 * g + h:4 * g + h + 1], None, ALU.add, None, [zero_b, se_b], [sexp_b])

        def load_kv(g, col0, ncols, with_ctx):
            kt, kb = attK[kbi[0] % 2]
            vt, vb = attV[kbi[0] % 2]
            kbi[0] += 1
            off = 0
            if with_ctx:
                P.dma("sp", kt[:, 0:PAST], kctxT_b[(l * NKV + g) * 128:(l * NKV + g + 1) * 128, :], [dbuf("ctx_b")], [kb], sembuf=kb)
                P.dma("sp", vt[:, 0:PAST // 128, :], vctx_b[l * PAST:(l + 1) * PAST, g * 128:(g + 1) * 128].rearrange("(c p) d -> p c d", p=128), [dbuf("ctx_b")], [vb], sembuf=vb)
                off = PAST
            rbk = [dbuf("kT%d" % l, b) for b in range(col0 // 512, (col0 + ncols) // 512)]
            rbv = [dbuf("v%d" % l, b) for b in range(col0 // 512, (col0 + ncols) // 512)]
            P.dma("sp", kt[:, off:off + ncols], kT_d[l][g * 128:(g + 1) * 128, col0:col0 + ncols], rbk, [kb], sembuf=kb)
            P.dma("sp", vt[:, off // 128:(off + ncols) // 128, :], v_d[l][col0:col0 + ncols, g * 128:(g + 1) * 128].rearrange("(c p) d -> p c d", p=128), rbv, [vb], sembuf=vb)
            return kt, kb, vt, vb, off

        def load_q(g, col0, i):
            qt, qb = attQ[i % 2]
            rb = [dbuf("qT%d" % l, b) for b in range(col0 // 512, (col0 + 1024) // 512)]
            P.dma("sp", qt[:], qT_d[l][4 * g * 128:(4 * g + 4) * 128, col0:col0 + 1024].rearrange("(h d) t -> d h t", h=4), rb, [qb], sembuf=qb)
            return qt, qb

        qi_ctr = [0]
        osi = [0]
        for g in range(NKV):
            set_sexp(g)
            pcols = TP
            kt, kb, vt, vb, off = load_kv(g, 0, pcols, False)
            nsq = c.SEQP // 128
            for q0 in range(0, TP, 1024):
                qw = min(1024, TP - q0)
                qt, qb = attQ[qi_ctr[0] % 2]
                qi_ctr[0] += 1
                rb = [dbuf("qT%d" % l, b) for b in range(q0 // 512, (q0 + qw) // 512)]
                P.dma("sp", qt[:, :, 0:qw], qT_d[l][4 * g * 128:(4 * g + 4) * 128, q0:q0 + qw].rearrange("(h d) t -> d h t", h=4), rb, [qb], sembuf=qb)
                qbs = []
                for sq_i in range(qw // c.SEQP):
                    seq0 = q0 + sq_i * c.SEQP
                    ost, osb = ostg[osi[0] % 2]
                    osi[0] += 1
                    for qq in range(nsq):
                        tiles = [(kt[:, seq0 + kk * 128:seq0 + (kk + 1) * 128], vt[:, (seq0 // 128) + kk, :], None, [kb, vb]) for kk in range(nsq)]

                        def fin(po, rd, rdb, ost=ost, osb=osb, qq=qq, seq0=seq0, last=(qq == nsq - 1)):
                            tt("dve", ost[:, :, qq * 128:(qq + 1) * 128], ps[:, po, :].rearrange("p (h q) -> p h q", h=4), rd[:].rearrange("p (h q) -> p h q", h=4), ALU.mult, [psb[po], rdb], [osb])
                            if last:
                                P.dma("pool", oT_d[l][4 * g * 128:(4 * g + 4) * 128, seq0:seq0 + c.SEQP].rearrange("(h d) t -> d h t", h=4), ost[:, :, 0:c.SEQP], [osb], [dbuf("oT%d" % l, seq0 // 512)], sembuf=osb)
                        lq = (seq0 - q0) + qq * 128
                        qbs.append(dict(q=(qt[:, :, lq:lq + 128], qb), tiles=tiles, g=g, fin=fin))
                run_stream(qbs)
            kt, kb, vt, vb, off = load_kv(g, TP, S, True)
            for qc in range(S // 1024):
                qt, qb = load_q(g, TP + qc * 1024, qi_ctr[0])
                qi_ctr[0] += 1
                qbs = []
                for qq in range(8):
                    i = qc * 8 + qq
                    if qq % 4 == 0:
                        ost, osb = ostg[osi[0] % 2]
                        osi[0] += 1
                    tiles = [(kt[:, kk * 128:(kk + 1) * 128], vt[:, kk, :], None, [kb, vb]) for kk in range(PAST // 128)]
                    if windowed:
                        for kk, mk in ((i - 1, mask_t[:, 0:512]), (i, None), (i + 1, mask_t[:, 512:1024])):
                            if 0 <= kk < S // 128:
                                tiles.append((kt[:, PAST + kk * 128:PAST + (kk + 1) * 128], vt[:, PAST // 128 + kk, :], mk, [kb, vb]))
                    else:
                        for kk in range(S // 128):
                            tiles.append((kt[:, PAST + kk * 128:PAST + (kk + 1) * 128], vt[:, PAST // 128 + kk, :], None, [kb, vb]))

                    def fin(po, rd, rdb, ost=ost, osb=osb, qq=qq, i=i):
                        q4 = qq % 4
                        tt("dve", ost[:, :, q4 * 128:(q4 + 1) * 128], ps[:, po, :].rearrange("p (h q) -> p h q", h=4), rd[:].rearrange("p (h q) -> p h q", h=4), ALU.mult, [psb[po], rdb], [osb])
                        if q4 == 3:
                            t0 = TP + (i - 3) * 128
                            P.dma("pool", oT_d[l][4 * g * 128:(4 * g + 4) * 128, t0:t0 + 512].rearrange("(h d) t -> d h t", h=4), ost[:], [osb], [dbuf("oT%d" % l, t0 // 512)], sembuf=osb)
                    qbs.append(dict(q=(qt[:, :, qq * 128:(qq + 1) * 128], qb), tiles=tiles, g=g, fin=fin))
                run_stream(qbs)
        P.barrier()

    def phase_C(l):
        xin, xinn = xs_d[2 * l], "x%d" % (2 * l)
        xout, xoutn = xs_d[2 * l + 1], "x%d" % (2 * l + 1)
        wdb = dbuf("wo_b%d" % l)
        for blk in range(NB):
            cd = 0 if blk < NBP else 1
            c0 = blk * 512
            ot_, ob_ = hT[blk % 2]
            P.dma("sp", ot_[:, :, 2:514], oT_d[l][:, c0:c0 + 512].rearrange("(h d) t -> d h t", h=NH), [dbuf("oT%d" % l, blk)], [ob_], sembuf=ob_)
            for ch in range(KC):
                r0 = (l * KC + ch) * 128
                wt, wb = load_w(wo_b[r0:r0 + 128, :], KC * 128, wdb)
                pa = PS_A[ch % 2]
                fns = [mm(ps[:, pa, :], wt[:, k * 128:(k + 1) * 128], ot_[:, k, 2:514], k == 0, k == KC - 1) for k in range(KC)]
                P.op("pe", fns, [wb, ob_], [psb[pa]])
                xt, xb = xsl.next()
                P.dma("sp", xt[:, 0:512], xin[ch * 128:(ch + 1) * 128, c0:c0 + 512], [dbuf(xinn, blk * KC + ch)], [xb], sembuf=xb)
                st, sb_ = stg_f32.next()
                stt(st[:], ps[:, pa, :], gates(l, cd, 0, ch), xt[:, 0:512], ALU.mult, ALU.add, [psb[pa], xb, modT_b], [sb_])
                P.dma("pool", xout[ch * 128:(ch + 1) * 128, c0:c0 + 512], st[:], [sb_], [dbuf(xoutn, blk * KC + ch)], sembuf=sb_)
                if l == 0 and ch % 4 == 0:
                    bg_step_l0()
        P.barrier()

    PS_G = [0, 1]
    PS_U = [2, 5]
    PS_GH = 3
    PS_D = [6, 7]

    def phase_D(l):
        xin, xinn = xs_d[2 * l + 1], "x%d" % (2 * l + 1)
        xout, xoutn = xs_d[2 * l + 2], "x%d" % (2 * l + 2)
        wgdb, wddb = dbuf("wgu_b%d" % l), dbuf("wd_b%d" % l)
        last_layer = (l == DEPTH - 1)

        def halo_flags(blk):
            smp = blk >= NBP
            return (smp and blk > NBP), (smp and blk < NB - 1)

        def mkprep(blk):
            hl_, hr_ = halo_flags(blk)
            return prep(xin, xinn, blk, l, 1, hT[blk % 2], halo_l=hl_, halo_r=hr_)

        drive(mkprep(0))
        for blk in range(NB):
            ht, hb = hT[blk % 2]
            sample = blk >= NBP
            cd = 1 if sample else 0
            c0 = blk * 512
            hl, hr = halo_flags(blk)
            gens = []
            if blk + 1 < NB:
                gens.append(mkprep(blk + 1))
            if last_layer and blk >= 1:
                gens.append(prep(xout, xoutn, blk - 1, 0, 0, hT[0], final=True))

            def side():
                while gens:
                    try:
                        next(gens[0])
                        return
                    except StopIteration:
                        gens.pop(0)

            nsq, sw = (1, 512) if sample else (512 // c.SEQP, c.SEQP)
            for part in range(c.NFP):
                f0, f1 = part * FH, min(FC, (part + 1) * FH)
                for f in range(f0, f1):
                    r0 = (l * FC + f) * 128
                    wgt, wgb = load_w(wg_b[r0:r0 + 128, :], KC * 128, wgdb)
                    wut, wub = load_w(wu_b[r0:r0 + 128, :], KC * 128, wgdb)
                    pg, pu = PS_G[f % 2], PS_U[f % 2]
                    fns = [mm(ps[:, pg, :], wgt[:, k * 128:(k + 1) * 128], ht[:, k, 2:514], k == 0, k == KC - 1) for k in range(KC)]
                    P.op("pe", fns, [wgb, hb], [psb[pg]])
                    if hl or hr:
                        fns = [mm(ps[:, PS_GH, 0:2], wgt[:, k * 128:(k + 1) * 128], ht[:, k, 0:516:514], k == 0, k == KC - 1) for k in range(KC)]
                        P.op("pe", fns, [wgb, hb], [psb[PS_GH]])
                    fns = [mm(ps[:, pu, :], wut[:, k * 128:(k + 1) * 128], ht[:, k, 2:514], k == 0, k == KC - 1) for k in range(KC)]
                    P.op("pe", fns, [wub, hb], [psb[pu]])
                    at, ab = f32a.next()
                    a3 = at[:, 0:nsq * (sw + 2)].rearrange("p (s w) -> p s w", s=nsq)
                    pg3 = ps[:, pg, :].rearrange("p (s w) -> p s w", s=nsq)
                    act(a3[:, :, 1:sw + 1], pg3, AF.Copy, [psb[pg]], [ab])
                    if hl or hr:
                        act(at[:, 0:514:513], ps[:, PS_GH, 0:2], AF.Copy, [psb[PS_GH]], [ab])
                    else:
                        P.op("pool", lambda e, a3=a3, sw=sw: e.memset(a3[:, :, 0:sw + 2:sw + 1], 0.0), [], [ab])
                    cw = lambda tap: convw_t[:, l, tap, f:f + 1]
                    t1, t1b = f32b.next()
                    t13 = t1[:].rearrange("p (s w) -> p s w", s=nsq)
                    ts("dve", t13, a3[:, :, 0:sw], cw(0), None, ALU.mult, None, [ab, convw_b], [t1b])
                    stt(t13, pg3, cw(1), t13, ALU.mult, ALU.add, [psb[pg], t1b, convw_b], [t1b])
                    stt(t13, a3[:, :, 2:sw + 2], cw(2), t13, ALU.mult, ALU.add, [ab, t1b, convw_b], [t1b])
                    s1, s1b = f32b.next()
                    act(s1[:], t1[:], AF.Silu, [t1b, convb_b], [s1b], bias=convb_t[:, l, f:f + 1])
                    tt("dve", gT_t[:, f - f0, :], s1[:], ps[:, pu, :], ALU.mult, [s1b, psb[pu]], [gT_b])
                    bg_step(1)
                    side()
                nf = f1 - f0
                for ch in range(KC):
                    r0 = (l * KC + ch) * 128
                    wt, wb = load_w(wd_b[r0:r0 + 128, f0 * 128:f1 * 128], nf * 128, wddb)
                    pd = PS_D[ch % 2]
                    fns = [mm(ps[:, pd, :], wt[:, k * 128:(k + 1) * 128], gT_t[:, k, :], k == 0, k == nf - 1) for k in range(nf)]
                    P.op("pe", fns, [wb, gT_b], [psb[pd]])
                    xt, xb = xsl.next()
                    srcx, srcn = (xin, xinn) if part == 0 else (xout, xoutn)
                    P.dma("sp", xt[:, 0:512], srcx[ch * 128:(ch + 1) * 128, c0:c0 + 512], [dbuf(srcn, blk * KC + ch)], [xb], sembuf=xb)
                    st, sb_ = stg_f32.next()
                    stt(st[:], ps[:, pd, :], gates(l, cd, 1, ch), xt[:, 0:512], ALU.mult, ALU.add, [psb[pd], xb, modT_b], [sb_])
                    P.dma("pool", xout[ch * 128:(ch + 1) * 128, c0:c0 + 512], st[:], [sb_], [dbuf(xoutn, blk * KC + ch)], sembuf=sb_)
                    bg_step(1)
                    side()
            while gens:
                side()
        if last_layer:
            drive(prep(xout, xoutn, NB - 1, 0, 0, hT[0], final=True))
        P.barrier()

    def phase_E():
        src, srcn = xs_d[2 * DEPTH], "x%d" % (2 * DEPTH)
        for blk in range(NB):
            drive(prep(src, srcn, blk, 0, 0, hT[0], final=True))
        P.barrier()

    import os as _os
    stop = int(_os.environ.get("KSTOP", "99"))
    steps = []
    for l in range(DEPTH):
        steps.append(lambda l=l: phase_A(l))
        steps.append(lambda l=l: phase_B(l))
        steps.append(lambda l=l: (bg_flush_until("wo%d" % l), phase_C(l)))
        steps.append(lambda l=l: (bg_flush_until("wgu%d" % l), bg_flush_until("wd%d" % l), phase_D(l),
                                  bg_flush_until("wqk%d" % (l + 1)) if l + 1 < DEPTH else None))
    for i, st_ in enumerate(steps):
        if i >= stop:
            break
        st_()
    while bg and stop >= 99:
        bg_step()
    P.barrier(final=True)

    with nc.Block() as block:
        P.replay(block)
    return nc


def _ws(W, K, N):
    return np.ascontiguousarray(W.reshape(K // 128, 128, N // 128, 128).transpose(2, 1, 0, 3)).reshape(N, K)


def _fm(v, nchunk):
    return np.ascontiguousarray(v.reshape(nchunk, 128).T)


def _cfg_from(x_prompt, x_sample, cache_k, w_gate):
    BATCH, SEQ, D = x_prompt.shape
    DEC_BATCH, DEC_SEQ, _ = x_sample.shape
    _, DEPTH, PAST, NKV, HD = cache_k.shape
    assert DEC_BATCH == NCORES and HD == 128
    return Cfg(D=D, NH=D // 128, NKV=NKV, DFF=w_gate.shape[2], SEQP=SEQ, NPSEQ=BATCH // NCORES, S=DEC_SEQ, PAST=PAST, DEPTH=DEPTH)


def kernel(x_prompt, x_sample, cache_k, cache_v, c, c_ctx, w_mod, b_mod, norm_attn, norm_ffn,
           w_qkv, w_o, sink_a, q_norm_b, k_norm_b, w_gate, w_up, w_down, conv_w, conv_b, norm_f):
    f = np.float32
    A = lambda a: np.asarray(a, dtype=f)
    x_prompt, x_sample, cache_k, cache_v = A(x_prompt), A(x_sample), A(cache_k), A(cache_v)
    cfg = _cfg_from(x_prompt, x_sample, cache_k, np.asarray(w_gate))
    D, KC, NH, NKV, FC, DEPTH = cfg.D, cfg.KC, cfg.NH, cfg.NKV, cfg.FC, cfg.DEPTH
    KVW, CBW, NCB, KG, NKG, S, TP, PAST = cfg.KVW, cfg.CBW, cfg.NCB, cfg.KG, cfg.NKG, cfg.S, cfg.TP, cfg.PAST
    QD = NH * 128
    w_mod, w_qkv, w_o, w_gate, w_up, w_down = A(w_mod), A(w_qkv), A(w_o), A(w_gate), A(w_up), A(w_down)

    shared = {}
    shared["wmod"] = np.concatenate([_ws(w_mod[l], D, 6 * D) for l in range(DEPTH)], 0)
    shared["bmod"] = np.concatenate([_fm(A(b_mod)[l], 6 * KC) for l in range(DEPTH)], 1)
    nl = []
    for l in range(DEPTH):
        nl += [_fm(A(norm_attn)[l], KC), _fm(A(norm_ffn)[l], KC)]
    nl.append(_fm(A(norm_f), KC))
    shared["nrm"] = np.concatenate(nl, 1)
    shared["wqk"] = np.concatenate([_ws(w_qkv[l][:, :QD + KVW], D, QD + KVW) for l in range(DEPTH)], 0)
    tl = []
    for l in range(DEPTH):
        for mat in range(2):
            M = w_qkv[l][:, QD + mat * KVW:QD + (mat + 1) * KVW]
            tl.append(np.ascontiguousarray(M.reshape(NKG, KG, 128, NCB, CBW).transpose(3, 0, 2, 1, 4)).reshape(NCB * NKG * 128, KG * CBW))
    shared["wkvt"] = np.concatenate(tl, 0)
    shared["wo"] = np.concatenate([_ws(w_o[l], QD, D) for l in range(DEPTH)], 0)
    shared["wg"] = np.concatenate([_ws(w_gate[l], D, cfg.DFF) for l in range(DEPTH)], 0)
    shared["wu"] = np.concatenate([_ws(w_up[l], D, cfg.DFF) for l in range(DEPTH)], 0)
    shared["wd"] = np.concatenate([_ws(w_down[l], cfg.DFF, D) for l in range(DEPTH)], 0)
    cw = A(conv_w)
    shared["convw"] = np.concatenate([_fm(cw[l, t], FC) for l in range(DEPTH) for t in range(3)], 1)
    shared["convb"] = np.concatenate([_fm(A(conv_b)[l], FC) for l in range(DEPTH)], 1)
    shared["sink"] = np.ascontiguousarray(np.broadcast_to(A(sink_a)[0][None, :], (128, NH)))
    shared["qkg"] = np.ascontiguousarray(np.stack([A(q_norm_b)[0], A(k_norm_b)[0]], 1))
    shared["kgb"] = np.ascontiguousarray(np.broadcast_to(A(k_norm_b)[0][None, :], (128, 128)))
    pos = np.arange(S)
    row, col = pos // GRID_W, pos % GRID_W
    half = 32
    inv = ROPE_THETA ** (-np.arange(half, dtype=np.float64) / half)
    C = np.zeros((128, S), np.float64)
    Sg = np.zeros((128, S), np.float64)
    for base, pp in ((0, row), (64, col)):
        ang = pp[None, :] * inv[:, None]
        C[base:base + 32] = np.cos(ang)
        C[base + 32:base + 64] = np.cos(ang)
        Sg[base:base + 32] = -np.sin(ang)
        Sg[base + 32:base + 64] = np.sin(ang)
    shared["ropeC"] = C.astype(f)
    shared["ropeS"] = Sg.astype(f)
    pm = np.zeros((128, 128), f)
    for m in range(128):
        blk, r = m // 32, m % 32
        pm[(blk ^ 1) * 32 + r, m] = 1.0
    shared["perm"] = pm
    a = np.arange(128)
    mprev = (a[None, :] <= a[:, None]).astype(f)
    mnext = (a[:, None] <= a[None, :]).astype(f)
    shared["masks"] = np.concatenate([np.tile(mprev, (1, 4)), np.tile(mnext, (1, 4))], 1)

    in_maps = []
    for core in range(NCORES):
        m = dict(shared)
        xp = x_prompt[core * cfg.NPSEQ:(core + 1) * cfg.NPSEQ].reshape(TP, D)
        xs = x_sample[core]
        m["xT"] = np.ascontiguousarray(np.concatenate([xp, xs], 0).T)
        cc = np.stack([A(c_ctx), A(c)[core]], 1)
        m["cT"] = np.ascontiguousarray(cc.reshape(KC, 128, 2).transpose(1, 0, 2)).reshape(128, KC * 2)
        ck = cache_k[core]
        m["kctxT"] = np.ascontiguousarray(ck.transpose(0, 2, 3, 1)).reshape(DEPTH * KVW, PAST)
        m["vctx"] = np.ascontiguousarray(cache_v[core].reshape(DEPTH * PAST, KVW))
        in_maps.append(m)

    nc = build(cfg)
    res = run_bass_kernel_spmd(nc, in_maps, core_ids=list(range(NCORES)))
    BATCH = x_prompt.shape[0]
    y_prompt = np.empty((BATCH, cfg.SEQP, D), f)
    y_sample = np.empty((NCORES, S, D), f)
    ctx_k = np.empty((BATCH, DEPTH, cfg.SEQP, NKV, 128), f)
    ctx_v = np.empty((BATCH, DEPTH, cfg.SEQP, NKV, 128), f)
    for core in range(NCORES):
        r = res.results[core]
        y = np.asarray(r["yT"]).T
        y_prompt[core * cfg.NPSEQ:(core + 1) * cfg.NPSEQ] = y[:TP].reshape(cfg.NPSEQ, cfg.SEQP, D)
        y_sample[core] = y[TP:]
        for nm, dst in (("ctxk", ctx_k), ("ctxv", ctx_v)):
            t = np.asarray(r[nm]).reshape(DEPTH, cfg.NPSEQ, cfg.SEQP, NKV, 128).transpose(1, 0, 2, 3, 4)
            dst[core * cfg.NPSEQ:(core + 1) * cfg.NPSEQ] = t
    return (y_prompt, y_sample, ctx_k, ctx_v)
```

```python
import math
import numpy as np
import concourse.bass as bass
import concourse.mybir as mybir
from concourse.bass_utils import run_bass_kernel_spmd

F32 = mybir.dt.float32
BF16 = mybir.dt.bfloat16
AF = mybir.ActivationFunctionType
ALU = mybir.AluOpType
AX = mybir.AxisListType

EPS = 1e-6
ROPE_THETA = 10000.0
GRID_W = 64
WINDOW = 128
NCORES = 8
INF = float("inf")


class Cfg:
    def __init__(s, D, NH, NKV, DFF, SEQP, NPSEQ, S, PAST, DEPTH=2):
        s.D, s.NH, s.NKV, s.DFF, s.SEQP, s.NPSEQ, s.S, s.PAST, s.DEPTH = D, NH, NKV, DFF, SEQP, NPSEQ, S, PAST, DEPTH
        s.KC = D // 128
        s.FC = DFF // 128
        s.NFP = 3
        s.FH = (s.FC + s.NFP - 1) // s.NFP
        s.TP = SEQP * NPSEQ
        s.TT = s.TP + S
        s.NB = s.TT // 512
        s.NBP = s.TP // 512
        s.NCH = NH + NKV
        s.KVW = NKV * 128
        s.CBW = min(512, s.KVW)
        s.NCB = s.KVW // s.CBW
        s.KG = min(8, s.KC)
        s.NKG = s.KC // s.KG
        s.KM = min(16, s.KC)
        s.NKM = s.KC // s.KM
        s.NPC = PAST // 128
        s.NSC = S // 128
        assert NH // NKV == 4 and s.TP % 512 == 0 and S % 1024 == 0 and NH == s.KC
        assert 512 % SEQP == 0


class Eng:
    def __init__(s, name):
        s.name, s.items, s.count, s.waited = name, [], 0, {}


class Buf:
    def __init__(s, name):
        s.name = name
        s.w = {}
        s.r = {}
        s.semkey = None
        s.excl = False
        s.semcount = 0
        s.last_dma = None


class Group:
    def __init__(s, key):
        s.key, s.total = key, 0


class Prog:
    def __init__(s, nc):
        s.nc = nc
        s.engs = {n: Eng(n) for n in ("pe", "act", "dve", "pool", "sp")}
        s.sems = {}
        for n in s.engs:
            s.sems[("E", n)] = nc.alloc_semaphore(name="e_" + n)
        s.bufs = []
        s.groups = {}
        s.nsem = 0
        s.tag = ""
        s.sb_off = (int(nc.sbuf_base) + 63) // 64 * 64
        s.sb_top = int(nc.sbuf_top)

    def sbuf(s, name, shape, dtype, off=None):
        esz = 4 if dtype == F32 else 2
        n = 1
        for d in shape[1:]:
            n *= d
        nbytes = (n * esz + 63) // 64 * 64
        if off is None:
            off = s.sb_off
            s.sb_off += nbytes
            assert s.sb_off <= s.sb_top, ("SBUF overflow", name, s.sb_off, s.sb_top)
        else:
            assert off + nbytes <= s.sb_top, ("SBUF overflow", name)
        return s.nc.alloc_sbuf_tensor_at(name, list(shape), dtype, offset=off)

    def buf(s, name, dma=False):
        b = Buf(name)
        if dma:
            s.nsem += 1
            b.semkey = ("S", s.nsem)
            s.sems[b.semkey] = s.nc.alloc_semaphore(name="d_%d" % s.nsem)
        s.bufs.append(b)
        return b

    def group(s, name):
        g = Group(("G", name))
        s.sems[g.key] = s.nc.alloc_semaphore(name="g_" + name)
        s.groups[g.key] = g
        return g

    @staticmethod
    def _deps(reads, writes):
        d = {}
        for b in reads:
            for k, v in b.w.items():
                if d.get(k, 0) < v:
                    d[k] = v
            if b.excl:
                for k, v in b.r.items():
                    if d.get(k, 0) < v:
                        d[k] = v
        for b in writes:
            for k, v in b.w.items():
                if d.get(k, 0) < v:
                    d[k] = v
            for k, v in b.r.items():
                if d.get(k, 0) < v:
                    d[k] = v
        return d

    @staticmethod
    def _filter(E, deps, skip=None):
        out = []
        for k, v in deps.items():
            if k == skip:
                continue
            if E.waited.get(k, 0) >= v:
                continue
            E.waited[k] = v
            out.append((k, v))
        return out

    @staticmethod
    def _mark(tok, reads, writes):
        k, v = tok
        for b in writes:
            b.w = {k: v}
            b.r = {}
        for b in reads:
            if b not in writes:
                b.r[k] = v

    def op(s, e, fns, reads=(), writes=()):
        if not isinstance(fns, (list, tuple)):
            fns = [fns]
        E = s.engs[e]
        waits = s._filter(E, s._deps(reads, writes), skip=("E", "pe") if e == "pe" else None)
        E.count += 1
        tok = (("E", e), E.count)
        n = len(fns)
        for i, f in enumerate(fns):
            E.items.append((waits if i == 0 else (), f, (tok[0], 1) if i == n - 1 else None, s.tag))
        s._mark(tok, reads, writes)
        return tok

    def dma(s, q, out, in_, reads, writes, sembuf=None, group=None, **kw):
        E = s.engs[q]
        deps = s._deps(reads, writes)
        if group is not None:
            deps.pop(group.key, None)
            group.total += 16
            tok = (group.key, INF)
            semkey = group.key
        else:
            if sembuf.last_dma is not None:
                k, v = sembuf.last_dma
                if deps.get(k, 0) < v:
                    deps[k] = v
            sembuf.semcount += 16
            tok = (sembuf.semkey, sembuf.semcount)
            sembuf.last_dma = tok
            semkey = sembuf.semkey
        waits = s._filter(E, deps)
        E.items.append((waits, lambda eng: eng.dma_start(out=out, in_=in_, **kw), (semkey, 16), s.tag))
        s._mark(tok, reads, writes)
        return tok

    def barrier(s, final=False):
        deps = {}
        if final:
            for k, g in s.groups.items():
                if g.total:
                    deps[k] = INF
        for n, E in s.engs.items():
            if E.count:
                deps[("E", n)] = E.count
        for b in s.bufs:
            if b.semkey is not None and b.semcount:
                deps[b.semkey] = b.semcount
        for n, E in s.engs.items():
            w = s._filter(E, dict(deps), skip=("E", n))
            if w:
                E.items.append((w, None, None, "barrier"))
        for b in s.bufs:
            b.w = {k: v for k, v in b.w.items() if k[0] == "G"}
            b.r = {}

    def replay(s, block):
        nc = s.nc

        def run(E, eng):
            for waits, fn, sig, _tag in E.items:
                for k, v in waits:
                    if v == INF:
                        v = s.groups[k].total
                    eng.wait_ge(s.sems[k], int(v))
                if fn is None:
                    continue
                inst = fn(eng)
                if sig is not None:
                    inst.then_inc(s.sems[sig[0]], sig[1])

        @block.tensor
        def _(eng):
            run(s.engs["pe"], eng)

        @block.scalar
        def _(eng):
            run(s.engs["act"], eng)

        @block.vector
        def _(eng):
            run(s.engs["dve"], eng)

        @block.gpsimd
        def _(eng):
            run(s.engs["pool"], eng)

        @block.sync
        def _(eng):
            run(s.engs["sp"], eng)


class Ring:
    def __init__(s, items):
        s.items, s.i = items, 0

    def next(s):
        it = s.items[s.i % len(s.items)]
        s.i += 1
        return it


def build(cfg):
    nc = bass.Bass("TRN2", target_bir_lowering=False)
    P = Prog(nc)
    c = cfg
    D, KC, NH, NKV, FC, FH, TT, TP, S, NB, NBP = c.D, c.KC, c.NH, c.NKV, c.FC, c.FH, c.TT, c.TP, c.S, c.NB, c.NBP
    DEPTH, NCH, KVW, CBW, NCB, KG, NKG, KM, NKM, PAST = c.DEPTH, c.NCH, c.KVW, c.CBW, c.NCB, c.KG, c.NKG, c.KM, c.NKM, c.PAST
    scale = 128 ** -0.5

    def din(name, shape, dt=F32):
        return nc.dram_tensor(name, list(shape), dt, kind="ExternalInput").ap()

    def dout(name, shape, dt=F32):
        return nc.dram_tensor(name, list(shape), dt, kind="ExternalOutput").ap()

    def dint(name, shape, dt):
        return nc.dram_tensor(name, list(shape), dt, kind="Internal").ap()

    xT = din("xT", [D, TT])
    cT = din("cT", [128, KC * 2])
    wmod = din("wmod", [DEPTH * 6 * KC * 128, D])
    bmod = din("bmod", [128, DEPTH * 6 * KC])
    nrm = din("nrm", [128, (2 * DEPTH + 1) * KC])
    wqk = din("wqk", [DEPTH * NCH * 128, D])
    wkvt = din("wkvt", [DEPTH * 2 * NCB * NKG * 128, KG * CBW])
    wo = din("wo", [DEPTH * KC * 128, D])
    wg = din("wg", [DEPTH * FC * 128, D])
    wu = din("wu", [DEPTH * FC * 128, D])
    wd = din("wd", [DEPTH * KC * 128, FC * 128])
    convw = din("convw", [128, DEPTH * 3 * FC])
    convb = din("convb", [128, DEPTH * FC])
    sinkd = din("sink", [128, NH])
    qkg = din("qkg", [128, 2])
    kgb = din("kgb", [128, 128])
    ropeC = din("ropeC", [128, S])
    ropeS = din("ropeS", [128, S])
    kctxT = din("kctxT", [DEPTH * KVW, PAST])
    vctx = din("vctx", [DEPTH * PAST, KVW])
    masksd = din("masks", [128, 1024])
    permd = din("perm", [128, 128])
    yT = dout("yT", [D, TT])
    ctxk = dout("ctxk", [DEPTH * TP, KVW])
    ctxv = dout("ctxv", [DEPTH * TP, KVW])

    wqk_b = dint("wqk_b", [DEPTH * NCH * 128, D], BF16)
    wkvt_b = dint("wkvt_b", [DEPTH * 2 * NCB * NKG * 128, KG * CBW], BF16)
    wo_b = dint("wo_b", [DEPTH * KC * 128, D], BF16)
    wg_b = dint("wg_b", [DEPTH * FC * 128, D], BF16)
    wu_b = dint("wu_b", [DEPTH * FC * 128, D], BF16)
    wd_b = dint("wd_b", [DEPTH * KC * 128, FC * 128], BF16)
    kctxT_b = dint("kctxT_b", [DEPTH * KVW, PAST], BF16)
    vctx_b = dint("vctx_b", [DEPTH * PAST, KVW], BF16)
    xs_d = [xT] + [dint("xres%d" % i, [D, TT], F32) for i in range(2 * DEPTH)]
    qT_d = [dint("qT%d" % l, [NH * 128, TT], BF16) for l in range(DEPTH)]
    kT_d = [dint("kT%d" % l, [KVW, TT], BF16) for l in range(DEPTH)]
    v_d = [dint("v%d" % l, [TT, KVW], BF16) for l in range(DEPTH)]
    oT_d = [dint("oT%d" % l, [NH * 128, TT], BF16) for l in range(DEPTH)]

    dbufs = {}

    def dbuf(name, blk=0):
        k = (name, blk)
        if k not in dbufs:
            dbufs[k] = P.buf("dram_%s_%d" % (name, blk))
        return dbufs[k]

    ps = nc.alloc_psum_tensor("ps", [128, 8, 512], F32)
    psb = [P.buf("psum%d" % i) for i in range(8)]
    for b_ in psb:
        b_.excl = True

    def tile(name, shape, dt, dma=False, off=None):
        return P.sbuf(name, shape, dt, off), P.buf(name, dma=dma)

    ones_t, ones_b = tile("ones", [128, 128], BF16)
    perm_t, perm_b = tile("perm", [128, 128], BF16)
    mask_t, mask_b = tile("mask", [128, 1024], BF16)
    zero_t, zero_b = tile("zero", [128, 128], F32)
    onesf_t, onesf_b = tile("onesf", [128, 128], F32)
    cgrp = P.group("const")
    c_t, c_b = tile("cT", [128, KC, 2], F32)
    csil_t, csil_b = tile("csil", [128, KC, 2], F32)
    bmod_t, bmod_b = tile("bmod", [128, DEPTH, 6 * KC], F32)
    nrm_t, nrm_b = tile("nrm", [128, 2 * DEPTH + 1, KC], F32)
    convw_t, convw_b = tile("convw", [128, DEPTH, 3, FC], F32)
    convb_t, convb_b = tile("convb", [128, DEPTH, FC], F32)
    sink_t, sink_b = tile("sink", [128, NH], F32)
    qkg_t, qkg_b = tile("qkg", [128, 2], F32)
    kgb_t, kgb_b = tile("kgb", [128, 128], F32)
    modT_t, modT_b = tile("modT", [128, DEPTH, 2, 6 * KC], F32)
    eff_t, eff_b = tile("eff", [128, DEPTH, 2, 2, KC], F32)
    negB_t, negB_b = tile("negB", [128, 1], F32)
    se_t, se_b = tile("se", [128, NH], F32)

    WSLOT = max(KC * 128, FH * 128, KG * CBW)
    wring = Ring([tile("w%d" % i, [128, WSLOT], BF16, dma=True) for i in range(5)])
    xsl = Ring([tile("xsl%d" % i, [128, 514], F32, dma=True) for i in range(3)])
    sq = Ring([tile("sq%d" % i, [128, 512], BF16) for i in range(2)])
    f32a = Ring([tile("fa%d" % i, [128, 516], F32) for i in range(4)])
    f32b = Ring([tile("fb%d" % i, [128, 512], F32) for i in range(3)])
    bfa = Ring([tile("ba%d" % i, [128, 512], BF16) for i in range(3)])
    stg_b16 = Ring([tile("sb%d" % i, [128, 512], BF16, dma=True) for i in range(3)])
    stg_f32 = Ring([tile("sf%d" % i, [128, 512], F32, dma=True) for i in range(3)])
    rope_r = Ring([(tile("rc%d" % i, [128, 512], F32, dma=True), tile("rs%d" % i, [128, 512], F32, dma=True)) for i in range(2)])
    t1r = Ring([tile("t1r%d" % i, [128, 512], F32) for i in range(3)])
    small_t, small_b = tile("small", [128, 2 * KC + 64], F32)
    small2_t, small2_b = tile("small2", [128, 2 * KC + 32], F32)
    SM = 2 * KC

    BIG = P.sb_off
    HTB = ((KC * 516 * 2 + 63) // 64) * 64
    hT = [(P.sbuf("hT%d" % i, [128, KC, 516], BF16, BIG + i * HTB), P.buf("hT%d" % i, dma=True)) for i in range(2)]
    gT_t = P.sbuf("gT", [128, FH, 512], BF16, BIG + 2 * HTB)
    gT_b = P.buf("gT")
    assert BIG + 2 * HTB + FH * 1024 <= P.sb_top, ("BIG overflow", BIG, HTB, FH, P.sb_top)
    KL = PAST + max(S, TP)
    o = BIG
    attQ = []
    for i in range(2):
        attQ.append((P.sbuf("aQ%d" % i, [128, 4, 1024], BF16, o), P.buf("aQ%d" % i, dma=True)))
        o += 8192
    attK = []
    for i in range(2):
        attK.append((P.sbuf("aK%d" % i, [128, KL], BF16, o), P.buf("aK%d" % i, dma=True)))
        o += (KL * 2 + 63) // 64 * 64
    attV = []
    for i in range(2):
        attV.append((P.sbuf("aV%d" % i, [128, KL // 128, 128], BF16, o), P.buf("aV%d" % i, dma=True)))
        o += (KL * 2 + 63) // 64 * 64
    ptr = []
    for i in range(3):
        ptr.append((P.sbuf("pt%d" % i, [128, 512], BF16, o), P.buf("pt%d" % i)))
        o += 1024
    ostg = []
    for i in range(2):
        ostg.append((P.sbuf("os%d" % i, [128, 4, 512], BF16, o), P.buf("os%d" % i, dma=True)))
        o += 4096
    sexp_t = P.sbuf("sexp", [128, 512], F32, o)
    sexp_b = P.buf("sexp")
    o += 2048
    assert o <= P.sb_top, "attention overlay overflow"

    def mm(out, lhsT, rhs, start, stop):
        return lambda e: e.matmul(out, lhsT=lhsT, rhs=rhs, start=start, stop=stop)

    def act(out, in_, func, reads, writes, bias=None, scale=None):
        kw = {}
        if bias is not None:
            kw["bias"] = bias
        if scale is not None:
            kw["scale"] = scale
        P.op("act", lambda e: e.activation(out=out, in_=in_, func=func, **kw), reads, writes)

    def ts(eng, out, in0, s1, s2, op0, op1, reads, writes):
        if op1 is None:
            P.op(eng, lambda e: e.tensor_scalar(out=out, in0=in0, scalar1=s1, scalar2=None, op0=op0), reads, writes)
        else:
            P.op(eng, lambda e: e.tensor_scalar(out=out, in0=in0, scalar1=s1, scalar2=s2, op0=op0, op1=op1), reads, writes)

    def stt(out, in0, scalar, in1, op0, op1, reads, writes):
        P.op("dve", lambda e: e.scalar_tensor_tensor(out=out, in0=in0, scalar=scalar, in1=in1, op0=op0, op1=op1), reads, writes)

    def tt(eng, out, in0, in1, op, reads, writes):
        P.op(eng, lambda e: e.tensor_tensor(out=out, in0=in0, in1=in1, op=op), reads, writes)

    groups = {}
    bg = []

    def add_cast(gname, dst, src, rows, dbname):
        if gname not in groups:
            groups[gname] = P.group(gname)
        g = groups[gname]
        R = dst.shape[0]
        for r0 in range(0, R, rows):
            r1 = min(R, r0 + rows)
            bg.append((g, dst[r0:r1, :], src[r0:r1, :], dbname))

    import os as _os2
    KDBG = int(_os2.environ.get("KDBG", "0"))

    def bg_step(n=1):
        if KDBG & 1:
            bg.clear()
            return
        for _ in range(n):
            if not bg:
                return
            g, dst, src, dbname = bg.pop(0)
            P.dma("pool", dst, src, [], [dbuf(dbname)], group=g)

    def bg_step_l0():
        if bg and (bg[0][0].key[1].endswith("0") or bg[0][0].key[1] == "ctx"):
            bg_step(1)

    def bg_flush_until(gname):
        while any(g is groups[gname] for g, _, _, _ in bg):
            bg_step()

    for l in range(DEPTH):
        add_cast("wqk%d" % l, wqk_b[l * NCH * 128:(l + 1) * NCH * 128, :], wqk[l * NCH * 128:(l + 1) * NCH * 128, :], 128, "wqk_b%d" % l)
        n = 2 * NCB * NKG * 128
        add_cast("wqk%d" % l, wkvt_b[l * n:(l + 1) * n, :], wkvt[l * n:(l + 1) * n, :], 128, "wqk_b%d" % l)
        if l == 0:
            add_cast("ctx", kctxT_b, kctxT, 128, "ctx_b")
            add_cast("ctx", vctx_b, vctx, 128, "ctx_b")
        add_cast("wo%d" % l, wo_b[l * KC * 128:(l + 1) * KC * 128, :], wo[l * KC * 128:(l + 1) * KC * 128, :], 128, "wo_b%d" % l)
        add_cast("wgu%d" % l, wg_b[l * FC * 128:(l + 1) * FC * 128, :], wg[l * FC * 128:(l + 1) * FC * 128, :], 128, "wgu_b%d" % l)
        add_cast("wgu%d" % l, wu_b[l * FC * 128:(l + 1) * FC * 128, :], wu[l * FC * 128:(l + 1) * FC * 128, :], 128, "wgu_b%d" % l)
        add_cast("wd%d" % l, wd_b[l * KC * 128:(l + 1) * KC * 128, :], wd[l * KC * 128:(l + 1) * KC * 128, :], 128, "wd_b%d" % l)

    def cload(t, b, src):
        P.dma("sp", t, src, [], [b], group=cgrp)

    cload(c_t[:], c_b, cT.rearrange("p (k c) -> p k c", c=2))
    cload(bmod_t[:], bmod_b, bmod.rearrange("p (l m) -> p l m", l=DEPTH))
    cload(nrm_t[:], nrm_b, nrm.rearrange("p (n k) -> p n k", k=KC))
    cload(convw_t[:], convw_b, convw.rearrange("p (l t f) -> p l t f", l=DEPTH, t=3))
    cload(convb_t[:], convb_b, convb.rearrange("p (l f) -> p l f", l=DEPTH))
    cload(sink_t[:], sink_b, sinkd)
    cload(qkg_t[:], qkg_b, qkg)
    cload(kgb_t[:], kgb_b, kgb)
    P.dma("pool", perm_t[:], permd, [], [perm_b], group=cgrp)
    P.dma("pool", mask_t[:], masksd, [], [mask_b], group=cgrp)
    P.op("dve", lambda e: e.memset(ones_t[:], 1.0), [], [ones_b])
    P.op("dve", lambda e: e.memset(zero_t[:], 0.0), [], [zero_b])
    P.op("dve", lambda e: e.memset(onesf_t[:], 1.0), [], [onesf_b])
    P.op("dve", lambda e: e.memset(negB_t[:], 0.0), [], [negB_b])
    bg_flush_until("wqk0")
    bg_flush_until("ctx")

    act(csil_t[:], c_t[:], AF.Silu, [c_b], [csil_b])
    PS_M = 3
    KM = min(KM, WSLOT // 256)
    NKM = KC // KM

    def mod_gen(l, c_lo, c_hi, afs):
        for ch in range(c_lo, c_hi):
            col = 128 + (ch % 64) * 2
            for hm in range(NKM):
                wt, wb = wring.next()
                wv = wt[:, 0:2 * KM * 128].bitcast(F32).rearrange("p (k j) -> p k j", j=128)
                r0 = (l * 6 * KC + ch) * 128
                P.dma("sp", wv, wmod[r0:r0 + 128, hm * KM * 128:(hm + 1) * KM * 128].rearrange("p (k j) -> p k j", j=128), [], [wb], sembuf=wb)
                fns = [mm(ps[:, PS_M, col:col + 2], wv[:, k, :], csil_t[:, hm * KM + k, :], (hm == 0 and k == 0), (hm == NKM - 1 and k == KM - 1)) for k in range(KM)]
                P.op("pe", fns, [wb, csil_b], [psb[PS_M]])
            ts("dve", modT_t[:, l, :, ch], ps[:, PS_M, col:col + 2], bmod_t[:, l, ch:ch + 1], None, ALU.add, None, [psb[PS_M], bmod_b], [modT_b])
            yield
        for af in afs:
            for cd in range(2):
                stt(eff_t[:, l, cd, af, :], modT_t[:, l, cd, (3 * af + 1) * KC:(3 * af + 2) * KC], 1.0, nrm_t[:, 2 * l + af, :], ALU.add, ALU.mult, [modT_b, nrm_b], [eff_b])
        yield

    for _ in mod_gen(0, 0, 2 * KC, [0]):
        bg_step(1)
    bg_flush_until("wo0")
    mod_side = {0: mod_gen(0, 2 * KC, 6 * KC, [1])}
    if DEPTH > 1:
        mod_side[1] = mod_gen(1, 0, 6 * KC, [0, 1])
    P.barrier()

    def effs(l, cd, af, k):
        return eff_t[:, l, cd, af, k:k + 1]

    def shs(l, cd, af, k):
        return modT_t[:, l, cd, 3 * af * KC + k:3 * af * KC + k + 1]

    def gates(l, cd, af, k):
        return modT_t[:, l, cd, (3 * af + 2) * KC + k:(3 * af + 2) * KC + k + 1]

    PS_PREP = 4
    PS_PREPH = 3

    def prep(src, srcname, blk, l, af, hbuf, halo_l=False, halo_r=False, final=False):
        ht, hb = hbuf
        cd = 0 if blk < NBP else 1
        c0 = blk * 512
        lo, hi = c0 - (1 if halo_l else 0), c0 + 512 + (1 if halo_r else 0)
        so = 0 if halo_l else 1
        W = hi - lo
        def rbk(k):
            r = [dbuf(srcname, blk * KC + k)]
            if halo_l:
                r.append(dbuf(srcname, (blk - 1) * KC + k))
            if halo_r:
                r.append(dbuf(srcname, (blk + 1) * KC + k))
            return r
        halos = [(0, halo_l), (513, halo_r)]
        hcol = {0: 0, 513: 514}
        anyh = halo_l or halo_r
        for k in range(KC):
            xt, xb = xsl.next()
            P.dma("sp", xt[:, so:so + W], src[k * 128:(k + 1) * 128, lo:hi], rbk(k), [xb], sembuf=xb)
            st, sb_ = sq.next()
            act(st[:], xt[:, 1:513], AF.Square, [xb], [sb_])
            P.op("pe", mm(ps[:, PS_PREP, :], ones_t[:], st[:], k == 0, k == KC - 1), [sb_, ones_b], [psb[PS_PREP]])
            if anyh:
                for hi_, (col, on) in enumerate(halos):
                    if on:
                        act(small_t[:, 2 * k + hi_:2 * k + hi_ + 1], xt[:, col:col + 1], AF.Square, [xb], [small_b])
            yield
        rt, rtb = f32b.next()
        act(rt[:], ps[:, PS_PREP, :], AF.Sqrt, [psb[PS_PREP]], [rtb], bias=EPS, scale=1.0 / D)
        P.op("dve", lambda e: e.reciprocal(out=ps[:, PS_PREP, :], in_=rt[:]), [rtb], [psb[PS_PREP]])
        if anyh:
            for hi_, (col, on) in enumerate(halos):
                if on:
                    P.op("dve", lambda e, hi_=hi_: e.tensor_copy(out=small2_t[:, hi_ * KC:(hi_ + 1) * KC], in_=small_t[:, hi_:2 * KC:2]), [small_b], [small2_b])
                    P.op("dve", lambda e, hi_=hi_: e.tensor_reduce(out=small2_t[:, 2 * KC + 8 + hi_:2 * KC + 9 + hi_], in_=small2_t[:, hi_ * KC:(hi_ + 1) * KC], axis=AX.X, op=ALU.add), [small2_b], [small2_b])
                    P.op("pe", mm(ps[:, PS_PREPH, hi_:hi_ + 1], onesf_t[:], small2_t[:, 2 * KC + 8 + hi_:2 * KC + 9 + hi_], True, True), [small2_b, onesf_b], [psb[PS_PREPH]])
                    act(small2_t[:, 2 * KC + hi_:2 * KC + hi_ + 1], ps[:, PS_PREPH, hi_:hi_ + 1], AF.Sqrt, [psb[PS_PREPH]], [small2_b], bias=EPS, scale=1.0 / D)
                    P.op("dve", lambda e, hi_=hi_: e.reciprocal(out=small2_t[:, 2 * KC + 2 + hi_:2 * KC + 3 + hi_], in_=small2_t[:, 2 * KC + hi_:2 * KC + hi_ + 1]), [small2_b], [small2_b])
        yield
        for k in range(KC):
            xt, xb = xsl.next()
            P.dma("sp", xt[:, so:so + W], src[k * 128:(k + 1) * 128, lo:hi], rbk(k), [xb], sembuf=xb)
            if final:
                ot, ob = stg_f32.next()
                stt(ot[:], xt[:, 1:513], nrm_t[:, 2 * DEPTH, k:k + 1], ps[:, PS_PREP, :], ALU.mult, ALU.mult, [xb, psb[PS_PREP], nrm_b], [ob])
                P.dma("pool", yT[k * 128:(k + 1) * 128, c0:c0 + 512], ot[:], [ob], [dbuf("yT", blk)], sembuf=ob)
            else:
                tt_, tb = f32a.next()
                stt(tt_[:, 0:512], xt[:, 1:513], effs(l, cd, af, k), ps[:, PS_PREP, :], ALU.mult, ALU.mult, [xb, psb[PS_PREP], eff_b], [tb])
                act(ht[:, k, 2:514], tt_[:, 0:512], AF.Identity, [tb, modT_b], [hb], bias=shs(l, cd, af, k))
                for hi_, (col, on) in enumerate(halos):
                    if on:
                        P.op("dve", lambda e, hi_=hi_, col=col, k=k, xt=xt: e.tensor_scalar(
                            out=small_t[:, SM + hi_:SM + 1 + hi_], in0=xt[:, col:col + 1], scalar1=effs(l, cd, af, k),
                            scalar2=small2_t[:, 2 * KC + 2 + hi_:2 * KC + 3 + hi_], op0=ALU.mult, op1=ALU.mult), [xb, small2_b, eff_b], [small_b])
                        act(ht[:, k, hcol[col]:hcol[col] + 1], small_t[:, SM + hi_:SM + 1 + hi_], AF.Identity, [small_b, modT_b], [hb], bias=shs(l, cd, af, k))
            yield
        if not final:
            for hi_, (col, on) in enumerate(halos):
                if not on:
                    P.op("pool", lambda e, col=col: e.memset(ht[:, :, hcol[col]:hcol[col] + 1], 0.0), [], [hb])

    def drive(gen):
        for _ in gen:
            pass

    PS_A = [0, 1]
    PS_ROT = 2
    PS_SS = 3
    PS_V = [5, 6, 7, 2]

    def load_w(src_ap, nel, deps_db):
        wt, wb = wring.next()
        P.dma("sp", wt[:, 0:nel], src_ap, [deps_db], [wb], sembuf=wb)
        return wt, wb

    def phase_A(l):
        src, srcname = xs_d[2 * l], "x%d" % (2 * l)
        wdb = dbuf("wqk_b%d" % l)
        gen = prep(src, srcname, 0, l, 0, hT[0])
        drive(gen)
        KA = int(_os2.environ.get("KA", "0"))
        if KA == 1:
            P.barrier()
            return
        for blk in range(NB):
            ht, hb = hT[blk % 2]
            nxt = prep(src, srcname, blk + 1, l, 0, hT[(blk + 1) % 2]) if blk + 1 < NB else None
            sample = blk >= NBP and KA not in (3, 4)
            c0 = blk * 512
            if sample:
                (rc_t, rc_b), (rs_t, rs_b) = rope_r.next()
                s0 = c0 - TP
                P.dma("sp", rc_t[:], ropeC[:, s0:s0 + 512], [], [rc_b], sembuf=rc_b)
                P.dma("sp", rs_t[:], ropeS[:, s0:s0 + 512], [], [rs_b], sembuf=rs_b)
            PSA3 = [0, 1, 5]
            pend1, pend2 = [], []

            def stage0(ch):
                r0 = (l * NCH + ch) * 128
                wt, wb = load_w(wqk_b[r0:r0 + 128, :], KC * 128, wdb)
                pa = PSA3[ch % 3]
                fns = [mm(ps[:, pa, :], wt[:, k * 128:(k + 1) * 128], ht[:, k, 2:514], k == 0, k == KC - 1) for k in range(KC)]
                P.op("pe", fns, [wb, hb], [psb[pa]])
                return dict(ch=ch, pa=pa)

            def stage1(st_):
                ch, pa = st_["ch"], st_["pa"]
                isq = ch < NH
                cur, cur_b = ps[:, pa, :], psb[pa]
                st_["ot"] = None
                if l % 2 == 1:
                    sqt, sqb = sq.next()
                    act(sqt[:], cur, AF.Square, [cur_b], [sqb])
                    P.op("pe", mm(ps[:, PS_SS, :], ones_t[:], sqt[:], True, True), [sqb, ones_b], [psb[PS_SS]])
                    rt, rtb = f32b.next()
                    act(rt[:], ps[:, PS_SS, :], AF.Sqrt, [psb[PS_SS]], [rtb], bias=EPS, scale=1.0 / 128)
                    rs2, rs2b = f32b.next()
                    P.op("dve", lambda e, rs2=rs2, rt=rt: e.reciprocal(out=rs2[:], in_=rt[:]), [rtb], [rs2b])
                    gcol = qkg_t[:, 0:1] if isq else qkg_t[:, 1:2]
                    if sample:
                        qn, qnb = f32a.next()
                        stt(qn[:, 0:512], cur, gcol, rs2[:], ALU.mult, ALU.mult, [cur_b, rs2b, qkg_b], [qnb])
                        cur, cur_b = qn[:, 0:512], qnb
                    else:
                        ot, ob = stg_b16.next()
                        stt(ot[:], cur, gcol, rs2[:], ALU.mult, ALU.mult, [cur_b, rs2b, qkg_b], [ob])
                        st_["ot"] = (ot, ob)
                if sample:
                    qb_t, qb_b = bfa.next()
                    act(qb_t[:], cur, AF.Copy, [cur_b], [qb_b])
                    t1, t1b = t1r.next()
                    tt("dve", t1[:], cur, rc_t[:], ALU.mult, [cur_b, rc_b, qb_b], [t1b])
                    st_["qb"] = (qb_t, qb_b)
                    st_["t1"] = (t1, t1b)
                elif l % 2 == 0:
                    ot, ob = stg_b16.next()
                    act(ot[:], cur, AF.Copy, [cur_b], [ob])
                    st_["ot"] = (ot, ob)

            def stage2(st_):
                ch = st_["ch"]
                isq = ch < NH
                dst = (qT_d[l][ch * 128:(ch + 1) * 128, c0:c0 + 512] if isq else kT_d[l][(ch - NH) * 128:(ch - NH + 1) * 128, c0:c0 + 512])
                dname = "qT%d" % l if isq else "kT%d" % l
                if sample:
                    qb_t, qb_b = st_["qb"]
                    t1, t1b = st_["t1"]
                    ot, ob = stg_b16.next()
                    P.op("pe", mm(ps[:, PS_ROT, :], perm_t[:], qb_t[:], True, True), [qb_b, perm_b], [psb[PS_ROT]])
                    tt("dve", ps[:, PS_ROT, :], ps[:, PS_ROT, :], rs_t[:], ALU.mult, [psb[PS_ROT], rs_b], [psb[PS_ROT]])
                    tt("dve", ot[:], ps[:, PS_ROT, :], t1[:], ALU.add, [t1b, psb[PS_ROT]], [ob])
                else:
                    ot, ob = st_["ot"]
                P.dma("pool", dst, ot[:], [ob], [dbuf(dname, blk)], sembuf=ob)

            for ch in range(NCH + 2):
                if ch < NCH:
                    pend1.append(stage0(ch))
                if ch >= 1 and pend1 and pend1[0]["ch"] == ch - 1:
                    st_ = pend1.pop(0)
                    stage1(st_)
                    pend2.append(st_)
                if ch >= 2 and pend2 and pend2[0]["ch"] == ch - 2:
                    stage2(pend2.pop(0))
                if nxt is not None and KA not in (4, 5):
                    next(nxt, None)
                    next(nxt, None)
                bg_step(1 if (l == 0 and ch % 3 == 0 and KA not in (4, 5)) else 0)
                if l == 0 and ch % 2 == 0:
                    next(mod_side[0], None)
            assert not pend1 and not pend2
            for mat in ([] if KA in (2, 4, 5) else [0, 1] if blk < NBP else [1]):
                for cb in range(NCB):
                    for kg in range(NKG):
                        r0 = ((((l * 2 + mat) * NCB + cb) * NKG) + kg) * 128
                        wt, wb = load_w(wkvt_b[r0:r0 + 128, :], KG * CBW, wdb)
                        for t4 in range(4):
                            pv = PS_V[t4]
                            fns = [mm(ps[:, pv, 0:CBW], ht[:, kg * KG + k, 2 + t4 * 128:2 + (t4 + 1) * 128], wt[:, k * CBW:(k + 1) * CBW],
                                      (kg == 0 and k == 0), (kg == NKG - 1 and k == KG - 1)) for k in range(KG)]
                            P.op("pe", fns, [wb, hb], [psb[pv]])
                    for t4 in range(4):
                        pv = PS_V[t4]
                        tok0 = c0 + t4 * 128
                        if mat == 1:
                            ot, ob = stg_b16.next()
                            act(ot[:, 0:CBW], ps[:, pv, 0:CBW], AF.Copy, [psb[pv]], [ob])
                            P.dma("pool", v_d[l][tok0:tok0 + 128, cb * CBW:(cb + 1) * CBW], ot[:, 0:CBW], [ob], [dbuf("v%d" % l, blk)], sembuf=ob)
                        if blk < NBP:
                            of, ofb = stg_f32.next()
                            if mat == 0 and l % 2 == 1:
                                nh = CBW // 128
                                s2, s2b = f32a.next()
                                act(s2[:, 0:CBW], ps[:, pv, 0:CBW], AF.Square, [psb[pv]], [s2b])
                                P.op("dve", lambda e, s2=s2, nh=nh: e.tensor_reduce(out=small_t[:, SM + 8:SM + 8 + nh], in_=s2[:, 0:CBW].rearrange("p (h d) -> p h d", h=nh), axis=AX.X, op=ALU.add), [s2b], [small_b])
                                act(small_t[:, SM + 16:SM + 16 + nh], small_t[:, SM + 8:SM + 8 + nh], AF.Sqrt, [small_b], [small_b], bias=EPS, scale=1.0 / 128)
                                P.op("dve", lambda e, nh=nh: e.reciprocal(out=small_t[:, SM + 24:SM + 24 + nh], in_=small_t[:, SM + 16:SM + 16 + nh]), [small_b], [small_b])
                                for h in range(nh):
                                    stt(of[:, h * 128:(h + 1) * 128], ps[:, pv, h * 128:(h + 1) * 128], small_t[:, SM + 24 + h:SM + 25 + h], kgb_t[:], ALU.mult, ALU.mult, [psb[pv], small_b, kgb_b], [ofb])
                            else:
                                P.op("dve", lambda e, of=of, pv=pv: e.tensor_copy(out=of[:, 0:CBW], in_=ps[:, pv, 0:CBW]), [psb[pv]], [ofb])
                            od = ctxk if mat == 0 else ctxv
                            P.dma("pool", od[l * TP + tok0:l * TP + tok0 + 128, cb * CBW:(cb + 1) * CBW], of[:, 0:CBW], [ofb], [dbuf("ctxo", 0)], sembuf=ofb)
            if nxt is not None:
                drive(nxt)
        if l == 0:
            drive(mod_side[0])
        P.barrier()

    PS_S = [0, 1, 6]
    PS_O = [2, 3]
    PS_SUM = [4, 5]

    def phase_B(l):
        sinkon = (l % 2 == 0)
        windowed = (l % 2 == 0)
        if sinkon:
            act(se_t[:], sink_t[:], AF.Exp, [sink_b, negB_b], [se_b], bias=negB_t[:, 0:1])
        kbi = [0]

        def run_stream(qblocks):
            flat = []
            for qi, qbk in enumerate(qblocks):
                for j, t in enumerate(qbk["tiles"]):
                    flat.append((qi, j, t))
            n = len(flat)
            pts = [None] * n

            def emit_S(i):
                qi, j, (kap, vap, mk, bfs) = flat[i]
                qap, qbuf = qblocks[qi]["q"]
                P.op("pe", mm(ps[:, PS_S[i % 3], :], kap, qap, True, True), bfs + [qbuf], [psb[PS_S[i % 3]]])

            emit_S(0)
            if n > 1:
                emit_S(1)
            for i in range(n):
                qi, j, (kap, vap, mk, bfs) = flat[i]
                nt = len(qblocks[qi]["tiles"])
                if i + 2 < n:
                    emit_S(i + 2)
                pt, ptb = ptr[i % 3]
                act(pt[:], ps[:, PS_S[i % 3], :], AF.Exp, [psb[PS_S[i % 3]], negB_b], [ptb], bias=negB_t[:, 0:1], scale=scale)
                if mk is not None:
                    tt("pool", pt[:], pt[:], mk, ALU.mult, [ptb, mask_b], [ptb])
                po, psm = PS_O[qi % 2], PS_SUM[qi % 2]
                P.op("pe", [mm(ps[:, po, :], vap, pt[:], j == 0, j == nt - 1), mm(ps[:, psm, :], ones_t[:], pt[:], j == 0, j == nt - 1)],
                     bfs + [ptb, ones_b], [psb[po], psb[psm]])
                if j == nt - 1:
                    g = qblocks[qi]["g"]
                    den, denb = f32b.next()
                    if sinkon:
                        tt("dve", den[:], ps[:, psm, :], sexp_t[:], ALU.add, [psb[psm], sexp_b], [denb])
                    else:
                        P.op("dve", lambda e, den=den, psm=psm: e.tensor_copy(out=den[:], in_=ps[:, psm, :]), [psb[psm]], [denb])
                    rd, rdb = f32b.next()
                    P.op("dve", lambda e, rd=rd, den=den: e.reciprocal(out=rd[:], in_=den[:]), [denb], [rdb])
                    qblocks[qi]["fin"](po, rd, rdb)
                    if l == 0:
                        bg_step_l0()

        def set_sexp(g):
            if sinkon:
                for h in range(4):
                    ts("dve", sexp_t[:, h * 128:(h + 1) * 128], zero_t[:], se_t[:, 4 * g + h:4 * g + h + 1], None, ALU.add, None, [zero_b, se_b], [sexp_b])

        def load_kv(g, col0, ncols, with_ctx):
            kt, kb = attK[kbi[0] % 2]
            vt, vb = attV[kbi[0] % 2]
            kbi[0] += 1
            off = 0
            if with_ctx:
                P.dma("sp", kt[:, 0:PAST], kctxT_b[(l * NKV + g) * 128:(l * NKV + g + 1) * 128, :], [dbuf("ctx_b")], [kb], sembuf=kb)
                P.dma("sp", vt[:, 0:PAST // 128, :], vctx_b[l * PAST:(l + 1) * PAST, g * 128:(g + 1) * 128].rearrange("(c p) d -> p c d", p=128), [dbuf("ctx_b")], [vb], sembuf=vb)
                off = PAST
            rbk = [dbuf("kT%d" % l, b) for b in range(col0 // 512, (col0 + ncols) // 512)]
            rbv = [dbuf("v%d" % l, b) for b in range(col0 // 512, (col0 + ncols) // 512)]
            P.dma("sp", kt[:, off:off + ncols], kT_d[l][g * 128:(g + 1) * 128, col0:col0 + ncols], rbk, [kb], sembuf=kb)
            P.dma("sp", vt[:, off // 128:(off + ncols) // 128, :], v_d[l][col0:col0 + ncols, g * 128:(g + 1) * 128].rearrange("(c p) d -> p c d", p=128), rbv, [vb], sembuf=vb)
            return kt, kb, vt, vb, off

        def load_q(g, col0, i):
            qt, qb = attQ[i % 2]
            rb = [dbuf("qT%d" % l, b) for b in range(col0 // 512, (col0 + 1024) // 512)]
            P.dma("sp", qt[:], qT_d[l][4 * g * 128:(4 * g + 4) * 128, col0:col0 + 1024].rearrange("(h d) t -> d h t", h=4), rb, [qb], sembuf=qb)
            return qt, qb

        qi_ctr = [0]
        osi = [0]
        for g in range(NKV):
            set_sexp(g)
            pcols = TP
            kt, kb, vt, vb, off = load_kv(g, 0, pcols, False)
            nsq = c.SEQP // 128
            for q0 in range(0, TP, 1024):
                qw = min(1024, TP - q0)
                qt, qb = attQ[qi_ctr[0] % 2]
                qi_ctr[0] += 1
                rb = [dbuf("qT%d" % l, b) for b in range(q0 // 512, (q0 + qw) // 512)]
                P.dma("sp", qt[:, :, 0:qw], qT_d[l][4 * g * 128:(4 * g + 4) * 128, q0:q0 + qw].rearrange("(h d) t -> d h t", h=4), rb, [qb], sembuf=qb)
                qbs = []
                for sq_i in range(qw // c.SEQP):
                    seq0 = q0 + sq_i * c.SEQP
                    ost, osb = ostg[osi[0] % 2]
                    osi[0] += 1
                    for qq in range(nsq):
                        tiles = [(kt[:, seq0 + kk * 128:seq0 + (kk + 1) * 128], vt[:, (seq0 // 128) + kk, :], None, [kb, vb]) for kk in range(nsq)]

                        def fin(po, rd, rdb, ost=ost, osb=osb, qq=qq, seq0=seq0, last=(qq == nsq - 1)):
                            tt("dve", ost[:, :, qq * 128:(qq + 1) * 128], ps[:, po, :].rearrange("p (h q) -> p h q", h=4), rd[:].rearrange("p (h q) -> p h q", h=4), ALU.mult, [psb[po], rdb], [osb])
                            if last:
                                P.dma("pool", oT_d[l][4 * g * 128:(4 * g + 4) * 128, seq0:seq0 + c.SEQP].rearrange("(h d) t -> d h t", h=4), ost[:, :, 0:c.SEQP], [osb], [dbuf("oT%d" % l, seq0 // 512)], sembuf=osb)
                        lq = (seq0 - q0) + qq * 128
                        qbs.append(dict(q=(qt[:, :, lq:lq + 128], qb), tiles=tiles, g=g, fin=fin))
                run_stream(qbs)
            kt, kb, vt, vb, off = load_kv(g, TP, S, True)
            for qc in range(S // 1024):
                qt, qb = load_q(g, TP + qc * 1024, qi_ctr[0])
                qi_ctr[0] += 1
                qbs = []
                for qq in range(8):
                    i = qc * 8 + qq
                    if qq % 4 == 0:
                        ost, osb = ostg[osi[0] % 2]
                        osi[0] += 1
                    tiles = [(kt[:, kk * 128:(kk + 1) * 128], vt[:, kk, :], None, [kb, vb]) for kk in range(PAST // 128)]
                    if windowed:
                        for kk, mk in ((i - 1, mask_t[:, 0:512]), (i, None), (i + 1, mask_t[:, 512:1024])):
                            if 0 <= kk < S // 128:
                                tiles.append((kt[:, PAST + kk * 128:PAST + (kk + 1) * 128], vt[:, PAST // 128 + kk, :], mk, [kb, vb]))
                    else:
                        for kk in range(S // 128):
                            tiles.append((kt[:, PAST + kk * 128:PAST + (kk + 1) * 128], vt[:, PAST // 128 + kk, :], None, [kb, vb]))

                    def fin(po, rd, rdb, ost=ost, osb=osb, qq=qq, i=i):
                        q4 = qq % 4
                        tt("dve", ost[:, :, q4 * 128:(q4 + 1) * 128], ps[:, po, :].rearrange("p (h q) -> p h q", h=4), rd[:].rearrange("p (h q) -> p h q", h=4), ALU.mult, [psb[po], rdb], [osb])
                        if q4 == 3:
                            t0 = TP + (i - 3) * 128
                            P.dma("pool", oT_d[l][4 * g * 128:(4 * g + 4) * 128, t0:t0 + 512].rearrange("(h d) t -> d h t", h=4), ost[:], [osb], [dbuf("oT%d" % l, t0 // 512)], sembuf=osb)
                    qbs.append(dict(q=(qt[:, :, qq * 128:(qq + 1) * 128], qb), tiles=tiles, g=g, fin=fin))
                run_stream(qbs)
        P.barrier()

    def phase_C(l):
        xin, xinn = xs_d[2 * l], "x%d" % (2 * l)
        xout, xoutn = xs_d[2 * l + 1], "x%d" % (2 * l + 1)
        wdb = dbuf("wo_b%d" % l)
        for blk in range(NB):
            cd = 0 if blk < NBP else 1
            c0 = blk * 512
            ot_, ob_ = hT[blk % 2]
            P.dma("sp", ot_[:, :, 2:514], oT_d[l][:, c0:c0 + 512].rearrange("(h d) t -> d h t", h=NH), [dbuf("oT%d" % l, blk)], [ob_], sembuf=ob_)
            for ch in range(KC):
                r0 = (l * KC + ch) * 128
                wt, wb = load_w(wo_b[r0:r0 + 128, :], KC * 128, wdb)
                pa = PS_A[ch % 2]
                fns = [mm(ps[:, pa, :], wt[:, k * 128:(k + 1) * 128], ot_[:, k, 2:514], k == 0, k == KC - 1) for k in range(KC)]
                P.op("pe", fns, [wb, ob_], [psb[pa]])
                xt, xb = xsl.next()
                P.dma("sp", xt[:, 0:512], xin[ch * 128:(ch + 1) * 128, c0:c0 + 512], [dbuf(xinn, blk * KC + ch)], [xb], sembuf=xb)
                st, sb_ = stg_f32.next()
                stt(st[:], ps[:, pa, :], gates(l, cd, 0, ch), xt[:, 0:512], ALU.mult, ALU.add, [psb[pa], xb, modT_b], [sb_])
                P.dma("pool", xout[ch * 128:(ch + 1) * 128, c0:c0 + 512], st[:], [sb_], [dbuf(xoutn, blk * KC + ch)], sembuf=sb_)
                if l == 0 and ch % 4 == 0:
                    bg_step_l0()
        P.barrier()

    PS_G = [0, 1]
    PS_U = [2, 5]
    PS_GH = 3
    PS_D = [6, 7]

    def phase_D(l):
        xin, xinn = xs_d[2 * l + 1], "x%d" % (2 * l + 1)
        xout, xoutn = xs_d[2 * l + 2], "x%d" % (2 * l + 2)
        wgdb, wddb = dbuf("wgu_b%d" % l), dbuf("wd_b%d" % l)
        last_layer = (l == DEPTH - 1)

        def halo_flags(blk):
            smp = blk >= NBP
            return (smp and blk > NBP), (smp and blk < NB - 1)

        def mkprep(blk):
            hl_, hr_ = halo_flags(blk)
            return prep(xin, xinn, blk, l, 1, hT[blk % 2], halo_l=hl_, halo_r=hr_)

        drive(mkprep(0))
        for blk in range(NB):
            ht, hb = hT[blk % 2]
            sample = blk >= NBP
            cd = 1 if sample else 0
            c0 = blk * 512
            hl, hr = halo_flags(blk)
            gens = []
            if blk + 1 < NB:
                gens.append(mkprep(blk + 1))
            if last_layer and blk >= 1:
                gens.append(prep(xout, xoutn, blk - 1, 0, 0, hT[0], final=True))

            def side():
                while gens:
                    try:
                        next(gens[0])
                        return
                    except StopIteration:
                        gens.pop(0)

            nsq, sw = (1, 512) if sample else (512 // c.SEQP, c.SEQP)
            for part in range(c.NFP):
                f0, f1 = part * FH, min(FC, (part + 1) * FH)
                for f in range(f0, f1):
                    r0 = (l * FC + f) * 128
                    wgt, wgb = load_w(wg_b[r0:r0 + 128, :], KC * 128, wgdb)
                    wut, wub = load_w(wu_b[r0:r0 + 128, :], KC * 128, wgdb)
                    pg, pu = PS_G[f % 2], PS_U[f % 2]
                    fns = [mm(ps[:, pg, :], wgt[:, k * 128:(k + 1) * 128], ht[:, k, 2:514], k == 0, k == KC - 1) for k in range(KC)]
                    P.op("pe", fns, [wgb, hb], [psb[pg]])
                    if hl or hr:
                        fns = [mm(ps[:, PS_GH, 0:2], wgt[:, k * 128:(k + 1) * 128], ht[:, k, 0:516:514], k == 0, k == KC - 1) for k in range(KC)]
                        P.op("pe", fns, [wgb, hb], [psb[PS_GH]])
                    fns = [mm(ps[:, pu, :], wut[:, k * 128:(k + 1) * 128], ht[:, k, 2:514], k == 0, k == KC - 1) for k in range(KC)]
                    P.op("pe", fns, [wub, hb], [psb[pu]])
                    at, ab = f32a.next()
                    a3 = at[:, 0:nsq * (sw + 2)].rearrange("p (s w) -> p s w", s=nsq)
                    pg3 = ps[:, pg, :].rearrange("p (s w) -> p s w", s=nsq)
                    act(a3[:, :, 1:sw + 1], pg3, AF.Copy, [psb[pg]], [ab])
                    if hl or hr:
                        act(at[:, 0:514:513], ps[:, PS_GH, 0:2], AF.Copy, [psb[PS_GH]], [ab])
                    else:
                        P.op("pool", lambda e, a3=a3, sw=sw: e.memset(a3[:, :, 0:sw + 2:sw + 1], 0.0), [], [ab])
                    cw = lambda tap: convw_t[:, l, tap, f:f + 1]
                    t1, t1b = f32b.next()
                    t13 = t1[:].rearrange("p (s w) -> p s w", s=nsq)
                    ts("dve", t13, a3[:, :, 0:sw], cw(0), None, ALU.mult, None, [ab, convw_b], [t1b])
                    stt(t13, pg3, cw(1), t13, ALU.mult, ALU.add, [psb[pg], t1b, convw_b], [t1b])
                    stt(t13, a3[:, :, 2:sw + 2], cw(2), t13, ALU.mult, ALU.add, [ab, t1b, convw_b], [t1b])
                    s1, s1b = f32b.next()
                    act(s1[:], t1[:], AF.Silu, [t1b, convb_b], [s1b], bias=convb_t[:, l, f:f + 1])
                    tt("dve", gT_t[:, f - f0, :], s1[:], ps[:, pu, :], ALU.mult, [s1b, psb[pu]], [gT_b])
                    bg_step(1)
                    side()
                    if l == 0 and DEPTH > 1:
                        next(mod_side[1], None)
                nf = f1 - f0
                for ch in range(KC):
                    r0 = (l * KC + ch) * 128
                    wt, wb = load_w(wd_b[r0:r0 + 128, f0 * 128:f1 * 128], nf * 128, wddb)
                    pd = PS_D[ch % 2]
                    fns = [mm(ps[:, pd, :], wt[:, k * 128:(k + 1) * 128], gT_t[:, k, :], k == 0, k == nf - 1) for k in range(nf)]
                    P.op("pe", fns, [wb, gT_b], [psb[pd]])
                    xt, xb = xsl.next()
                    srcx, srcn = (xin, xinn) if part == 0 else (xout, xoutn)
                    P.dma("sp", xt[:, 0:512], srcx[ch * 128:(ch + 1) * 128, c0:c0 + 512], [dbuf(srcn, blk * KC + ch)], [xb], sembuf=xb)
                    st, sb_ = stg_f32.next()
                    stt(st[:], ps[:, pd, :], gates(l, cd, 1, ch), xt[:, 0:512], ALU.mult, ALU.add, [psb[pd], xb, modT_b], [sb_])
                    P.dma("pool", xout[ch * 128:(ch + 1) * 128, c0:c0 + 512], st[:], [sb_], [dbuf(xoutn, blk * KC + ch)], sembuf=sb_)
                    bg_step(1)
                    side()
                    if l == 0 and DEPTH > 1:
                        next(mod_side[1], None)
            while gens:
                side()
        if last_layer:
            drive(prep(xout, xoutn, NB - 1, 0, 0, hT[0], final=True))
        if l == 0 and DEPTH > 1:
            drive(mod_side[1])
        P.barrier()

    def phase_E():
        src, srcn = xs_d[2 * DEPTH], "x%d" % (2 * DEPTH)
        for blk in range(NB):
            drive(prep(src, srcn, blk, 0, 0, hT[0], final=True))
        P.barrier()

    import os as _os
    stop = int(_os.environ.get("KSTOP", "99"))
    steps = []
    for l in range(DEPTH):
        steps.append(lambda l=l: phase_A(l))
        steps.append(lambda l=l: phase_B(l))
        steps.append(lambda l=l: (bg_flush_until("wo%d" % l), phase_C(l)))
        steps.append(lambda l=l: (bg_flush_until("wgu%d" % l), bg_flush_until("wd%d" % l), phase_D(l),
                                  bg_flush_until("wqk%d" % (l + 1)) if l + 1 < DEPTH else None))
    for i, st_ in enumerate(steps):
        if i >= stop:
            break
        st_()
    while bg and stop >= 99:
        bg_step()
    P.barrier(final=True)

    with nc.Block() as block:
        P.replay(block)
    return nc


def _ws(W, K, N):
    return np.ascontiguousarray(W.reshape(K // 128, 128, N // 128, 128).transpose(2, 1, 0, 3)).reshape(N, K)


def _fm(v, nchunk):
    return np.ascontiguousarray(v.reshape(nchunk, 128).T)


def _cfg_from(x_prompt, x_sample, cache_k, w_gate):
    BATCH, SEQ, D = x_prompt.shape
    DEC_BATCH, DEC_SEQ, _ = x_sample.shape
    _, DEPTH, PAST, NKV, HD = cache_k.shape
    assert DEC_BATCH == NCORES and HD == 128
    return Cfg(D=D, NH=D // 128, NKV=NKV, DFF=w_gate.shape[2], SEQP=SEQ, NPSEQ=BATCH // NCORES, S=DEC_SEQ, PAST=PAST, DEPTH=DEPTH)


def kernel(x_prompt, x_sample, cache_k, cache_v, c, c_ctx, w_mod, b_mod, norm_attn, norm_ffn,
           w_qkv, w_o, sink_a, q_norm_b, k_norm_b, w_gate, w_up, w_down, conv_w, conv_b, norm_f):
    f = np.float32
    A = lambda a: np.asarray(a, dtype=f)
    x_prompt, x_sample, cache_k, cache_v = A(x_prompt), A(x_sample), A(cache_k), A(cache_v)
    cfg = _cfg_from(x_prompt, x_sample, cache_k, np.asarray(w_gate))
    D, KC, NH, NKV, FC, DEPTH = cfg.D, cfg.KC, cfg.NH, cfg.NKV, cfg.FC, cfg.DEPTH
    KVW, CBW, NCB, KG, NKG, S, TP, PAST = cfg.KVW, cfg.CBW, cfg.NCB, cfg.KG, cfg.NKG, cfg.S, cfg.TP, cfg.PAST
    QD = NH * 128
    w_mod, w_qkv, w_o, w_gate, w_up, w_down = A(w_mod), A(w_qkv), A(w_o), A(w_gate), A(w_up), A(w_down)

    shared = {}
    shared["wmod"] = np.concatenate([_ws(w_mod[l], D, 6 * D) for l in range(DEPTH)], 0)
    shared["bmod"] = np.concatenate([_fm(A(b_mod)[l], 6 * KC) for l in range(DEPTH)], 1)
    nl = []
    for l in range(DEPTH):
        nl += [_fm(A(norm_attn)[l], KC), _fm(A(norm_ffn)[l], KC)]
    nl.append(_fm(A(norm_f), KC))
    shared["nrm"] = np.concatenate(nl, 1)
    shared["wqk"] = np.concatenate([_ws(w_qkv[l][:, :QD + KVW], D, QD + KVW) for l in range(DEPTH)], 0)
    tl = []
    for l in range(DEPTH):
        for mat in range(2):
            M = w_qkv[l][:, QD + mat * KVW:QD + (mat + 1) * KVW]
            tl.append(np.ascontiguousarray(M.reshape(NKG, KG, 128, NCB, CBW).transpose(3, 0, 2, 1, 4)).reshape(NCB * NKG * 128, KG * CBW))
    shared["wkvt"] = np.concatenate(tl, 0)
    shared["wo"] = np.concatenate([_ws(w_o[l], QD, D) for l in range(DEPTH)], 0)
    shared["wg"] = np.concatenate([_ws(w_gate[l], D, cfg.DFF) for l in range(DEPTH)], 0)
    shared["wu"] = np.concatenate([_ws(w_up[l], D, cfg.DFF) for l in range(DEPTH)], 0)
    shared["wd"] = np.concatenate([_ws(w_down[l], cfg.DFF, D) for l in range(DEPTH)], 0)
    cw = A(conv_w)
    shared["convw"] = np.concatenate([_fm(cw[l, t], FC) for l in range(DEPTH) for t in range(3)], 1)
    shared["convb"] = np.concatenate([_fm(A(conv_b)[l], FC) for l in range(DEPTH)], 1)
    shared["sink"] = np.ascontiguousarray(np.broadcast_to(A(sink_a)[0][None, :], (128, NH)))
    shared["qkg"] = np.ascontiguousarray(np.stack([A(q_norm_b)[0], A(k_norm_b)[0]], 1))
    shared["kgb"] = np.ascontiguousarray(np.broadcast_to(A(k_norm_b)[0][None, :], (128, 128)))
    pos = np.arange(S)
    row, col = pos // GRID_W, pos % GRID_W
    half = 32
    inv = ROPE_THETA ** (-np.arange(half, dtype=np.float64) / half)
    C = np.zeros((128, S), np.float64)
    Sg = np.zeros((128, S), np.float64)
    for base, pp in ((0, row), (64, col)):
        ang = pp[None, :] * inv[:, None]
        C[base:base + 32] = np.cos(ang)
        C[base + 32:base + 64] = np.cos(ang)
        Sg[base:base + 32] = -np.sin(ang)
        Sg[base + 32:base + 64] = np.sin(ang)
    shared["ropeC"] = C.astype(f)
    shared["ropeS"] = Sg.astype(f)
    pm = np.zeros((128, 128), f)
    for m in range(128):
        blk, r = m // 32, m % 32
        pm[(blk ^ 1) * 32 + r, m] = 1.0
    shared["perm"] = pm
    a = np.arange(128)
    mprev = (a[None, :] <= a[:, None]).astype(f)
    mnext = (a[:, None] <= a[None, :]).astype(f)
    shared["masks"] = np.concatenate([np.tile(mprev, (1, 4)), np.tile(mnext, (1, 4))], 1)

    in_maps = []
    for core in range(NCORES):
        m = dict(shared)
        xp = x_prompt[core * cfg.NPSEQ:(core + 1) * cfg.NPSEQ].reshape(TP, D)
        xs = x_sample[core]
        m["xT"] = np.ascontiguousarray(np.concatenate([xp, xs], 0).T)
        cc = np.stack([A(c_ctx), A(c)[core]], 1)
        m["cT"] = np.ascontiguousarray(cc.reshape(KC, 128, 2).transpose(1, 0, 2)).reshape(128, KC * 2)
        ck = cache_k[core]
        m["kctxT"] = np.ascontiguousarray(ck.transpose(0, 2, 3, 1)).reshape(DEPTH * KVW, PAST)
        m["vctx"] = np.ascontiguousarray(cache_v[core].reshape(DEPTH * PAST, KVW))
        in_maps.append(m)

    nc = build(cfg)
    res = run_bass_kernel_spmd(nc, in_maps, core_ids=list(range(NCORES)))
    BATCH = x_prompt.shape[0]
    y_prompt = np.empty((BATCH, cfg.SEQP, D), f)
    y_sample = np.empty((NCORES, S, D), f)
    ctx_k = np.empty((BATCH, DEPTH, cfg.SEQP, NKV, 128), f)
    ctx_v = np.empty((BATCH, DEPTH, cfg.SEQP, NKV, 128), f)
    for core in range(NCORES):
        r = res.results[core]
        y = np.asarray(r["yT"]).T
        y_prompt[core * cfg.NPSEQ:(core + 1) * cfg.NPSEQ] = y[:TP].reshape(cfg.NPSEQ, cfg.SEQP, D)
        y_sample[core] = y[TP:]
        for nm, dst in (("ctxk", ctx_k), ("ctxv", ctx_v)):
            t = np.asarray(r[nm]).reshape(DEPTH, cfg.NPSEQ, cfg.SEQP, NKV, 128).transpose(1, 0, 2, 3, 4)
            dst[core * cfg.NPSEQ:(core + 1) * cfg.NPSEQ] = t
    return (y_prompt, y_sample, ctx_k, ctx_v)
```
